# Optimizing a Trainium2 kernel written in Bass

```python
import jax, jax.numpy as jnp
from jax import lax
import numpy as np

D_MODEL = 2048
BATCH = 16
SEQ = 256
DEPTH = 2
DEC_BATCH = 4
DEC_SEQ = 1024
PAST_LEN = 512

GRID_W = 64
HEAD_DIM = 128
ATT_HEADS = 8
ATT_KV_HEADS = 2
ATT_GROUPS = ATT_HEADS // ATT_KV_HEADS
ATT_W = ATT_HEADS * HEAD_DIM
ATT_KV_W = ATT_KV_HEADS * HEAD_DIM
ML_HEADS = 4
ML_DK = 128
ML_DV = 128
ML_W = ML_HEADS * ML_DV
FO_GROUPS = 4
FO_GC = 128
FO_W = FO_GROUPS * FO_GC
D_MIX = ATT_W + ML_W + FO_W
N_GATES = 4 * ML_HEADS
PROJ_SIZES = (ATT_W, ATT_KV_W, ATT_KV_W, ATT_W, ML_W, ML_W, ML_W, ML_W, ML_W, N_GATES, FO_W, FO_W)
D_PROJ = 2 * ATT_W + 2 * ATT_KV_W + 5 * ML_W + N_GATES + 2 * FO_W
CONV_W = 3
CHUNK = 64
Q_BLOCK = 128
ROPE_BASE = 10000.0
EPS = 1e-6

kernel_name = 'hybrid_attn_mlstm_fourier_diffusion_step'


def rms_norm(x, w):
    xf = x.astype(jnp.float32)
    y = xf * lax.rsqrt(jnp.mean(xf * xf, axis=-1, keepdims=True) + EPS)
    return (y * w.astype(jnp.float32)).astype(x.dtype)


def rope_2d(x, n_rows):
    half = x.shape[-1] // 2
    inv_freq = ROPE_BASE ** (-jnp.arange(0, half, 2, dtype=jnp.float32) / half)
    rows = jnp.repeat(jnp.arange(n_rows, dtype=jnp.float32), GRID_W)
    cols = jnp.tile(jnp.arange(GRID_W, dtype=jnp.float32), n_rows)
    xf = x.astype(jnp.float32)

    def rot(xh, pos):
        ang = pos[:, None] * inv_freq
        cos = jnp.cos(ang)[:, None, :]
        sin = jnp.sin(ang)[:, None, :]
        x1, x2 = jnp.split(xh, 2, axis=-1)
        return jnp.concatenate([x1 * cos - x2 * sin, x1 * sin + x2 * cos], axis=-1)

    out = jnp.concatenate([rot(xf[..., :half], rows), rot(xf[..., half:], cols)], axis=-1)
    return out.astype(x.dtype)


def short_conv(x, w, b):
    T = x.shape[1]
    pad = CONV_W // 2
    xp = jnp.pad(x, ((0, 0), (pad, pad), (0, 0)))
    return sum(xp[:, j:j + T] * w[j] for j in range(CONV_W)) + b


def mlstm_chunkwise(q, k, v, ig, lf, C0, n0, m0):
    B, H, T, _ = q.shape
    DV = v.shape[-1]
    nc = T // CHUNK
    to_chunks = lambda a: jnp.moveaxis(a.reshape(B, H, nc, CHUNK, *a.shape[3:]), 2, 0)
    causal = jnp.tril(jnp.ones((CHUNK, CHUNK), dtype=bool))

    def step(carry, inp):
        C, n, m = carry
        qc, kc, vc, ic, fc = inp
        b = jnp.cumsum(fc, axis=-1)
        dmat = jnp.where(causal, b[..., :, None] - b[..., None, :] + ic[..., None, :], -jnp.inf)
        inter = b + m[..., None]
        m_row = jnp.maximum(inter, jnp.max(dmat, axis=-1))
        s = jnp.einsum('bhld,bhsd->bhls', qc, kc) * jnp.exp(dmat - m_row[..., None])
        w_inter = jnp.exp(inter - m_row)
        num = w_inter[..., None] * jnp.einsum('bhld,bhde->bhle', qc, C) + jnp.einsum('bhls,bhse->bhle', s, vc)
        den = w_inter * jnp.einsum('bhld,bhd->bhl', qc, n) + jnp.sum(s, axis=-1)
        h = num / jnp.maximum(jnp.abs(den), jnp.exp(-m_row))[..., None]
        b_last = b[..., -1]
        g = b_last[..., None] - b + ic
        m_new = jnp.maximum(b_last + m, jnp.max(g, axis=-1))
        w_k = jnp.exp(g - m_new[..., None])
        decay = jnp.exp(b_last + m - m_new)
        C_new = decay[..., None, None] * C + jnp.einsum('bhs,bhsd,bhse->bhde', w_k, kc, vc)
        n_new = decay[..., None] * n + jnp.einsum('bhs,bhsd->bhd', w_k, kc)
        return (C_new, n_new, m_new), h

    init = (C0.astype(jnp.float32), n0.astype(jnp.float32), m0.astype(jnp.float32))
    (C, n, m), hs = lax.scan(step, init, (to_chunks(q), to_chunks(k), to_chunks(v), to_chunks(ig), to_chunks(lf)))
    h = jnp.moveaxis(hs, 0, 2).reshape(B, H, T, DV)
    return h, (C, n, m)


def mlstm_bidir(q, k, v, gates, C0, n0, m0):
    B, T = gates.shape[:2]
    g = gates.reshape(B, T, 4, ML_HEADS).transpose(2, 0, 3, 1)
    ig_f, lf_f = g[0], jax.nn.log_sigmoid(g[1])
    ig_b, lf_b = g[2], jax.nn.log_sigmoid(g[3])
    h_f, (Cf, nf, mf) = mlstm_chunkwise(q, k, v, ig_f, lf_f, C0[:, 0], n0[:, 0], m0[:, 0])
    h_b, (Cb, nb, mb) = mlstm_chunkwise(jnp.flip(q, 2), jnp.flip(k, 2), jnp.flip(v, 2),
                                        jnp.flip(ig_b, -1), jnp.flip(lf_b, -1), C0[:, 1], n0[:, 1], m0[:, 1])
    h = h_f + jnp.flip(h_b, 2)
    return h, (jnp.stack([Cf, Cb], axis=1), jnp.stack([nf, nb], axis=1), jnp.stack([mf, mb], axis=1))


def block_attention(q, k, v):
    B, T, KVH, G, HD = q.shape
    nb = T // Q_BLOCK
    qb = jnp.swapaxes(q.reshape(B, nb, Q_BLOCK, KVH, G, HD), 0, 1)

    def one_block(qi):
        s = jnp.einsum('bqhgd,bkhd->bhgqk', qi, k).astype(jnp.float32) * (HD ** -0.5)
        p = jax.nn.softmax(s, axis=-1).astype(v.dtype)
        return jnp.einsum('bhgqk,bkhd->bqhgd', p, v)

    o = lax.map(one_block, qb)
    return jnp.swapaxes(o, 0, 1).reshape(B, T, KVH * G * HD)


def mixer_layer(x, cond, lw, ctx_kv, ml_state, n_rows):
    (norm_w, w_mod, b_mod, w_in, b_if, conv_w, conv_b, q_norm, k_norm, ml_norm, w_fno, w_out) = lw
    B, T, _ = x.shape
    mod = jax.nn.silu(cond) @ w_mod + b_mod
    shift, scale, gate = jnp.split(mod[:, None, :], 3, axis=-1)
    h = rms_norm(x, norm_w) * (1 + scale) + shift
    z = h @ w_in
    (aq, ak, av, ag, mq, mk, mv, mo, mg, mif, fx, fg) = jnp.split(z, np.cumsum(PROJ_SIZES)[:-1].tolist(), axis=-1)

    q = rms_norm(aq.reshape(B, T, ATT_HEADS, HEAD_DIM), q_norm)
    k = rms_norm(ak.reshape(B, T, ATT_KV_HEADS, HEAD_DIM), k_norm)
    v = av.reshape(B, T, ATT_KV_HEADS, HEAD_DIM)
    if ctx_kv is None:
        k_all, v_all = k, v
    else:
        q = rope_2d(q, n_rows)
        k = rope_2d(k, n_rows)
        k_all = jnp.concatenate([ctx_kv[0].astype(k.dtype), k], axis=1)
        v_all = jnp.concatenate([ctx_kv[1].astype(v.dtype), v], axis=1)
    o_att = block_attention(q.reshape(B, T, ATT_KV_HEADS, ATT_GROUPS, HEAD_DIM), k_all, v_all) * jax.nn.silu(ag)

    qk = jax.nn.silu(short_conv(jnp.concatenate([mq, mk], axis=-1), conv_w, conv_b))
    mq_c, mk_c = jnp.split(qk, 2, axis=-1)
    heads = lambda a: a.reshape(B, T, ML_HEADS, -1).transpose(0, 2, 1, 3).astype(jnp.float32)
    gates = mif.astype(jnp.float32) + b_if.astype(jnp.float32)
    hm, (C_f, n_f, m_f) = mlstm_bidir(heads(mq_c), heads(mk_c) * (ML_DK ** -0.5), heads(mv), gates,
                                      ml_state[0], ml_state[1], ml_state[2])
    hm = rms_norm(hm.transpose(0, 2, 1, 3), ml_norm.reshape(ML_HEADS, ML_DV)).astype(x.dtype)
    o_ml = hm.reshape(B, T, ML_W) * jax.nn.sigmoid(mo) * jax.nn.silu(mg)

    f = jnp.fft.fftn(fx.reshape(B, T, FO_GROUPS, FO_GC).astype(jnp.float32), axes=(1, 3), norm='ortho').real
    o_fo = jnp.einsum('btgc,gcd->btgd', f.astype(x.dtype), w_fno).reshape(B, T, FO_W) * jax.nn.silu(fg)

    y = jnp.concatenate([o_att, o_ml, o_fo], axis=-1) @ w_out
    return x + gate * y, (k, v, C_f, n_f, m_f)


def setup_inputs(seed: int = 0) -> dict:
    key = jax.random.key(seed)
    ks = jax.random.split(key, 21)
    nrm = lambda i, shape, s: s * jax.random.normal(ks[i], shape, jnp.float32)
    f_off = jnp.tile(jnp.repeat(jnp.array([0.0, 3.0], jnp.float32), ML_HEADS), 2)
    return {
        'x_prompt': nrm(0, (BATCH, SEQ, D_MODEL), 1.0),
        'x_sample': nrm(1, (DEC_BATCH, DEC_SEQ, D_MODEL), 1.0),
        'cache_k': nrm(2, (DEC_BATCH, DEPTH, PAST_LEN, ATT_KV_HEADS, HEAD_DIM), 1.0),
        'cache_v': nrm(3, (DEC_BATCH, DEPTH, PAST_LEN, ATT_KV_HEADS, HEAD_DIM), 1.0),
        'state_C': nrm(4, (DEC_BATCH, DEPTH, 2, ML_HEADS, ML_DK, ML_DV), 0.5),
        'state_n': nrm(5, (DEC_BATCH, DEPTH, 2, ML_HEADS, ML_DK), 0.5),
        'state_m': nrm(6, (DEC_BATCH, DEPTH, 2, ML_HEADS), 1.0),
        'c': nrm(7, (DEC_BATCH, D_MODEL), 1.0),
        'c_ctx': nrm(8, (D_MODEL,), 1.0),
        'norm_w': 1.0 + nrm(9, (DEPTH, D_MODEL), 0.02),
        'w_mod': nrm(10, (DEPTH, D_MODEL, 3 * D_MODEL), 0.5 * D_MODEL ** -0.5),
        'b_mod': nrm(11, (DEPTH, 3 * D_MODEL), 0.02),
        'w_in': nrm(12, (DEPTH, D_MODEL, D_PROJ), D_MODEL ** -0.5),
        'b_if': f_off + nrm(13, (DEPTH, N_GATES), 0.1),
        'conv_w': nrm(14, (DEPTH, CONV_W, 2 * ML_W), CONV_W ** -0.5),
        'conv_b': nrm(15, (DEPTH, 2 * ML_W), 0.02),
        'q_norm': 1.0 + nrm(16, (DEPTH, HEAD_DIM), 0.02),
        'k_norm': 1.0 + nrm(17, (DEPTH, HEAD_DIM), 0.02),
        'ml_norm': 1.0 + nrm(18, (DEPTH, ML_W), 0.02),
        'w_fno': nrm(19, (DEPTH, FO_GROUPS, FO_GC, FO_GC), FO_GC ** -0.5),
        'w_out': nrm(20, (DEPTH, D_MIX, D_MODEL), D_MIX ** -0.5),
    }


def reference(x_prompt, x_sample, cache_k, cache_v, state_C, state_n, state_m, c, c_ctx,
              norm_w, w_mod, b_mod, w_in, b_if, conv_w, conv_b, q_norm, k_norm, ml_norm, w_fno, w_out):
    Bp = x_prompt.shape[0]
    zero_state = (jnp.zeros((Bp, 2, ML_HEADS, ML_DK, ML_DV), jnp.float32),
                  jnp.zeros((Bp, 2, ML_HEADS, ML_DK), jnp.float32),
                  jnp.zeros((Bp, 2, ML_HEADS), jnp.float32))
    xp = x_prompt
    new_k, new_v, new_C, new_n, new_m = [], [], [], [], []
    for l in range(DEPTH):
        lw = (norm_w[l], w_mod[l], b_mod[l], w_in[l], b_if[l], conv_w[l], conv_b[l],
              q_norm[l], k_norm[l], ml_norm[l], w_fno[l], w_out[l])
        xp, (k_l, v_l, C_l, n_l, m_l) = mixer_layer(xp, c_ctx[None, :], lw, None, zero_state, None)
        new_k.append(k_l)
        new_v.append(v_l)
        new_C.append(C_l)
        new_n.append(n_l)
        new_m.append(m_l)

    n_rows = x_sample.shape[1] // GRID_W
    xs = x_sample
    for l in range(DEPTH):
        lw = (norm_w[l], w_mod[l], b_mod[l], w_in[l], b_if[l], conv_w[l], conv_b[l],
              q_norm[l], k_norm[l], ml_norm[l], w_fno[l], w_out[l])
        xs, _ = mixer_layer(xs, c, lw, (cache_k[:, l], cache_v[:, l]),
                            (state_C[:, l], state_n[:, l], state_m[:, l]), n_rows)

    return (xp, xs, jnp.stack(new_k, axis=1), jnp.stack(new_v, axis=1), jnp.stack(new_C, axis=1),
            jnp.stack(new_n, axis=1), jnp.stack(new_m, axis=1))
```

```python
import numpy as np
import ml_dtypes
import concourse.bass as bass
import concourse.mybir as mybir
from concourse.bass_utils import run_bass_kernel_spmd

F32 = mybir.dt.float32
BF16 = mybir.dt.bfloat16
AF = mybir.ActivationFunctionType
ALU = mybir.AluOpType
AX = mybir.AxisListType
_ESZ = {F32: 4, BF16: 2}


def _is_ap(a):
    return hasattr(a, "tensor") and hasattr(a, "ap") and hasattr(a, "offset")


def _box(ap):
    t = ap.tensor
    name = t.name
    esz = _ESZ.get(ap.dtype, 4)
    tsz = _ESZ.get(t.dtype, 4)
    shape = list(t.shape)
    tps = 1
    for s in shape[1:]:
        tps *= int(s)
    tps_b = tps * tsz
    dims = [(int(s), int(c)) for s, c in ap.ap]
    off_b = int(ap.offset) * esz
    p0 = off_b // tps_b
    f0 = off_b % tps_b
    pstep, pcnt = dims[0]
    pstep_b = pstep * esz
    if pstep_b % tps_b == 0 and pstep_b > 0:
        p1 = p0 + (pcnt - 1) * (pstep_b // tps_b) + 1
        rest = dims[1:]
    elif pstep_b == 0:
        p1 = p0 + 1
        rest = dims[1:]
    else:
        p1 = p0 + 1
        rest = dims
    lo = f0
    hi = f0
    for s, c in rest:
        ext = s * (c - 1) * esz
        if ext < 0:
            lo += ext
        else:
            hi += ext
    hi += esz
    if t.__class__.__name__.startswith("PSum"):
        return name, (0, 128, (lo // 2048) * 2048, ((hi + 2047) // 2048) * 2048)
    return name, (p0, p1, lo, hi)


def _overlap(a, b):
    return a[0] < b[1] and b[0] < a[1] and a[2] < b[3] and b[2] < a[3]


def _contains(a, b):
    return a[0] <= b[0] and b[1] <= a[1] and a[2] <= b[2] and b[3] <= a[3]


class _Eng:
    def __init__(self, prog, name):
        self._p = prog
        self._n = name

    def __getattr__(self, op):
        def call(*args, **kw):
            return self._p._emit(self._n, op, args, kw)
        return call


class Prog:
    ENGS = ("pe", "act", "dve", "pool", "sp")

    def __init__(self, nc):
        self.nc = nc
        self.stream = {e: [] for e in self.ENGS}
        self.sems = {}
        self.count = {}
        self.waited = {e: {} for e in self.ENGS}
        self.dma_closed = {}
        self.writes = {}
        self.reads = {}
        self.pe_pending = False
        self._ctx = []
        self._npsum = 0
        for e in self.ENGS:
            self._sem(e)
        self.pe = _Eng(self, "pe")
        self.act = _Eng(self, "act")
        self.dve = _Eng(self, "dve")
        self.pool = _Eng(self, "pool")
        self._consts = {}
        self.ninstr = 0

    def _sem(self, key):
        if key not in self.sems:
            self.sems[key] = self.nc.alloc_semaphore(name="s_" + key.replace(":", "_"))
            self.count[key] = 0
        return self.sems[key]

    def sbuf(self, name, shape, dt):
        self._uid = getattr(self, "_uid", 0) + 1
        name = "%s_%d" % (name, self._uid)
        a0 = int(self.nc.sbuf_base)
        g = self.nc.sbuf_tensor(name, list(shape), dt)
        t = g.__enter__()
        a1 = int(self.nc.sbuf_base)
        self._ctx.append((g, name, a0, a1))
        inh = {}
        for (f0, f1, ticks) in getattr(self, "_freed", ()):
            if f0 < a1 and a0 < f1:
                for k, v in ticks.items():
                    inh[k] = max(inh.get(k, 0), v)
        if inh:
            tsz = 1
            for x in shape[1:]:
                tsz *= int(x)
            full = (0, 128, 0, tsz * _ESZ.get(dt, 4))
            self.reads[name] = [(full, k, v) for k, v in inh.items()]
        return t

    def psum(self, name, shape=(128, 512), dt=F32):
        g = self.nc.psum_tensor(name, list(shape), dt)
        t = g.__enter__()
        self._ctx.append((g, name, -1, -1))
        return t

    def _deps_read(self, ap, deps):
        if not _is_ap(ap) or ap.tensor.__class__.__name__.startswith("DRam"):
            return None
        name, bx = _box(ap)
        for (b, k, v) in self.writes.get(name, ()):
            if _overlap(b, bx):
                deps[k] = max(deps.get(k, 0), v)
        return name, bx

    def _deps_write(self, ap, deps):
        if not _is_ap(ap) or ap.tensor.__class__.__name__.startswith("DRam"):
            return None
        name, bx = _box(ap)
        for (b, k, v) in self.writes.get(name, ()):
            if _overlap(b, bx):
                deps[k] = max(deps.get(k, 0), v)
        for (b, k, v) in self.reads.get(name, ()):
            if _overlap(b, bx):
                deps[k] = max(deps.get(k, 0), v)
        return name, bx

    def _rec_read(self, nb, key, val):
        if nb is None:
            return
        name, bx = nb
        lst = self.reads.setdefault(name, [])
        lst[:] = [r for r in lst if not (r[1] == key and _contains(bx, r[0]))]
        lst.append((bx, key, val))

    def _rec_write(self, nb, key, val):
        if nb is None:
            return
        name, bx = nb
        w = self.writes.setdefault(name, [])
        w[:] = [r for r in w if not _contains(bx, r[0])]
        w.append((bx, key, val))
        r = self.reads.setdefault(name, [])
        r[:] = [x for x in r if not _contains(bx, x[0])]

    def _emit_waits(self, eng, deps):
        for k, v in deps.items():
            if k == "pe" and eng == "pe":
                continue
            if k == "pe" and self.pe_pending and v > self.count["pe"]:
                self._promote_pe()
            if k.startswith("dma:"):
                self.dma_closed[k] = True
                v = self.count[k]
            if self.waited[eng].get(k, 0) >= v:
                continue
            self.waited[eng][k] = v
            sem = self.sems[k]
            self.stream[eng].append(lambda e, sem=sem, v=v: e.wait_ge(sem, v))

    def defer_begin(self):
        self._defer = []

    def defer_end(self):
        ops, self._defer = self._defer, None
        return ops

    def replay(self, ops, n):
        for _ in range(n):
            if not ops:
                return
            kind, a = ops.pop(0)
            if kind == "e":
                self._emit(*a)
            else:
                self.dma(*a[0], **a[1])

    def _emit(self, eng, op, args, kw, inc=True):
        if getattr(self, "_defer", None) is not None:
            self._defer.append(("e", (eng, op, args, kw, inc)))
            return
        outs = []
        ins = []
        first = True
        for a in args:
            if _is_ap(a):
                if first:
                    outs.append(a)
                else:
                    ins.append(a)
                first = False
            elif first and not isinstance(a, (int, float)):
                first = False
        for k, a in kw.items():
            if _is_ap(a):
                if k in ("out", "accum_out"):
                    outs.append(a)
                else:
                    ins.append(a)
        deps = {}
        pin = [a for a in ins if _is_ap(a) and a.tensor.__class__.__name__.startswith("PSum")]
        ins = [a for a in ins if not (_is_ap(a) and a.tensor.__class__.__name__.startswith("PSum"))]
        outs = outs + pin
        rn = [self._deps_read(a, deps) for a in ins]
        wn = [self._deps_write(a, deps) for a in outs]
        self._emit_waits(eng, deps)
        sem = self.sems[eng]
        val = self.count[eng] + 1
        if inc:
            self.count[eng] = val
            if eng == "pe":
                self.pe_pending = False
            self.stream[eng].append(
                lambda e, op=op, args=args, kw=kw, sem=sem: getattr(e, op)(*args, **kw).then_inc(sem, 1))
        else:
            assert eng == "pe"
            self.pe_pending = True
            self._pe_last = (len(self.stream[eng]), op, args, kw)
            self.stream[eng].append(lambda e, op=op, args=args, kw=kw: getattr(e, op)(*args, **kw))
        for nb in rn:
            self._rec_read(nb, eng, val)
        for nb in wn:
            self._rec_write(nb, eng, val)
        self.ninstr += 1

    def _promote_pe(self):
        idx, op, args, kw = self._pe_last
        sem = self.sems["pe"]
        self.stream["pe"][idx] = (
            lambda e, op=op, args=args, kw=kw, sem=sem: getattr(e, op)(*args, **kw).then_inc(sem, 1))
        self.count["pe"] += 1
        self.pe_pending = False

    def mark(self, name):
        if not hasattr(self, "marks"):
            self.marks = []
        self.marks.append((name, getattr(self, "n_mm", 0)))

    def mm(self, out, lhsT, rhs, start=True, stop=True, inc=None):
        self.n_mm = getattr(self, "n_mm", 0) + 1
        if inc is None:
            inc = stop
        self._emit("pe", "matmul", (out, lhsT, rhs), dict(start=start, stop=stop), inc=inc)

    def transpose(self, out, in_, ident, inc=True):
        self.n_mm = getattr(self, "n_mm", 0) + 1
        self._emit("pe", "transpose", (out, in_, ident), {}, inc=inc)

    def dma(self, queue, out, in_, sem=None, slow=False):
        if getattr(self, "_defer", None) is not None:
            self._defer.append(("d", ((queue, out, in_), dict(sem=sem, slow=slow))))
            return
        sb = out if not out.tensor.__class__.__name__.startswith("DRam") else in_
        key = "dma:" + (sem if sem is not None else sb.tensor.name)
        s = self._sem(key)
        deps = {}
        rn = self._deps_read(in_, deps)
        wn = self._deps_write(out, deps)
        if self.dma_closed.get(key, False) and self.count[key] > 0:
            deps[key] = self.count[key]
        self._emit_waits(queue, deps)
        self.dma_closed[key] = False
        self.count[key] += 16
        val = self.count[key]
        kw = dict(allow_slow_non_contiguous=True) if slow else {}
        self.stream[queue].append(
            lambda e, out=out, in_=in_, s=s, kw=kw: e.dma_start(out=out, in_=in_, **kw).then_inc(s, 16))
        self._rec_read(rn, key, val)
        self._rec_write(wn, key, val)
        self.ninstr += 1

    def const_col(self, value):
        if value not in self._consts:
            t = self.sbuf("cst%d" % len(self._consts), [128, 1], F32)
            self.pool.memset(t[:], float(value))
            self._consts[value] = t
        return self._consts[value][:]

    def make_identity(self, t):
        n = int(t.shape[0])
        self.pool.memset(t[:], 1.0)
        self.pool.affine_select(t[:], t[:], [[-1, n]], ALU.is_equal, 0.0, base=0, channel_multiplier=1)

    def finish(self):
        for k in list(self.sems):
            if k.startswith("dma:") and self.count[k] > 0:
                sem, v = self.sems[k], self.count[k]
                self.stream["sp"].append(lambda e, sem=sem, v=v: e.wait_ge(sem, v))
        for k in ("pe", "act", "dve", "pool"):
            if self.count[k] > 0:
                sem, v = self.sems[k], self.count[k]
                self.stream["sp"].append(lambda e, sem=sem, v=v: e.wait_ge(sem, v))
        st = self.stream
        with self.nc.Block() as block:
            @block.sync
            def _(e):
                for f in st["sp"]:
                    f(e)

            @block.tensor
            def _(e):
                for f in st["pe"]:
                    f(e)

            @block.scalar
            def _(e):
                for f in st["act"]:
                    f(e)

            @block.vector
            def _(e):
                for f in st["dve"]:
                    f(e)

            @block.gpsimd
            def _(e):
                for f in st["pool"]:
                    f(e)
        for g in reversed(self._ctx):
            g[0].__exit__(None, None, None)
        self._ctx = []


    def barrier(self):
        if self.pe_pending:
            self._promote_pe()
        for e in self.ENGS:
            deps = {}
            for k in self.sems:
                if self.count[k] > 0:
                    deps[k] = self.count[k]
            self._emit_waits(e, deps)

    def scope_begin(self):
        return len(self._ctx)

    def scope_end(self, mark, barrier=False):
        if barrier:
            self.barrier()
        if self.pe_pending:
            self._promote_pe()
        if not hasattr(self, "_freed"):
            self._freed = []
        while len(self._ctx) > mark:
            g, name, a0, a1 = self._ctx.pop()
            ticks = {}
            for lst in (self.writes.pop(name, []), self.reads.pop(name, [])):
                for (_, k, v) in lst:
                    if k.startswith("dma:"):
                        v = self.count[k]
                    ticks[k] = max(ticks.get(k, 0), v)
            if ticks and a0 >= 0:
                self._freed.append((a0, a1, ticks))
            g.__exit__(None, None, None)


NT = 1024
KT = 16
C_AQ, C_AK, C_AV, C_AG = 0, 1024, 1280, 1536
C_MQ, C_MK, C_MV, C_MO, C_MG, C_MIF, C_FX, C_FG = 2560, 3072, 3584, 4096, 4608, 5120, 5136, 5648
EPS = 1e-6
ATT_SCALE = 128 ** -0.5
MLK_SCALE = 128 ** -0.5
NEG = -30000.0
LCH = 128
NCH = 1024 // LCH
import os as _os
_SKIP = _os.environ.get('KDBG_SKIP', '').split(',')
WIN_BLOCKS = ([(n, g) for g in range(2) for n in ("kv", "aq", "ag")]
              + [(n, 0) for n in ("mq", "mk", "mv", "mo", "mg", "fx", "fg")])
WIN_COL = dict(ag=C_AG, aq=C_AQ, mq=C_MQ, mk=C_MK, mv=C_MV, mo=C_MO, mg=C_MG, fx=C_FX, fg=C_FG)


def mod_queue(nlayers):
    q = [(0, c) for c in range(8, 12)]
    for l in range(1, nlayers):
        q += [(l, c) for c in range(12)]
    return q


def make_plan(nlayers):
    plan = [("mod", 0, c) for c in range(8)]
    q = mod_queue(nlayers)

    def slot():
        if q:
            plan.append(("mod",) + q.pop(0))

    for l in range(nlayers):
        for (n, g) in WIN_BLOCKS:
            plan.append(("in", l, n, g))
            slot()
            if (n, g) == ("ag", 1):
                for cc in range(4):
                    plan.append(("wo", l, 0, cc))
            if (n, g) == ("mv", 0):
                for _ in range(4):
                    slot()
        for cc in range(4):
            plan.append(("wo", l, 1, cc))
    return plan


class WStream:
    NSLOT = 4

    def __init__(self, p, plan, w_in_d, w_out_d, w_mod_d):
        self.p = p
        self.blocks = plan
        self.slots = [p.sbuf("ws%d" % i, [128, 4096], BF16) for i in range(self.NSLOT)]
        self.w_in_d, self.w_out_d, self.w_mod_d = w_in_d, w_out_d, w_mod_d
        self.chunks = []
        self.first = []
        for bi, b in enumerate(plan):
            self.first.append(len(self.chunks))
            n = 1 if (b[0] == "wo" or (b[0] == "in" and b[2] == "kv")) else 2
            for part in range(n):
                self.chunks.append((bi, part))
        self.nload = 0
        self.nuse = 0

    def _slot(self, ci):
        return self.slots[ci % self.NSLOT]

    def _load(self, ci):
        bi, part = self.chunks[ci]
        b = self.blocks[bi]
        s = self._slot(ci)
        p = self.p
        if b[0] == "in" and b[2] == "kv":
            _, l, _, g = b
            v = s[:, :].rearrange("p (k c) -> p k c", c=256)
            for (o, c0) in ((0, C_AK + g * 128), (128, C_AV + g * 128)):
                p.dma("pool", v[:, :, o:o + 128],
                      self.w_in_d[l][:, c0:c0 + 128].rearrange("(k p) c -> p k c", p=128))
            return
        if b[0] == "wo":
            _, l, part_, cc = b
            v = s[:, :].rearrange("p (k c) -> p k c", c=512)
            src = self.w_out_d[l][part_ * 1024:(part_ + 1) * 1024, cc * 512:(cc + 1) * 512]
        elif b[0] == "mod":
            _, l, ci_ = b
            v = s[:, :].rearrange("p (k c) -> p k c", c=256)
            src = self.w_mod_d[l][:, ci_ * 512 + part * 256:ci_ * 512 + (part + 1) * 256]
        else:
            _, l, n, g = b
            c0 = WIN_COL[n] + g * 512 + part * 256
            v = s[:, :].rearrange("p (k c) -> p k c", c=256)
            src = self.w_in_d[l][:, c0:c0 + 256]
        p.dma("pool", v, src.rearrange("(k p) c -> p k c", p=128))

    def advance(self):
        bi = self.nuse - 1
        if bi + 1 >= len(self.first) or self.first[bi + 1] - self.first[bi] != 2:
            return
        lim = min(len(self.chunks), self.first[bi] + self.NSLOT + 1)
        while self.nload < lim:
            self._load(self.nload)
            self.nload += 1

    def get(self, spec):
        bi = self.nuse
        assert self.blocks[bi] == spec, (self.blocks[bi], spec)
        self.nuse += 1
        c0 = self.first[bi]
        while self.nload < min(len(self.chunks), c0 + self.NSLOT):
            self._load(self.nload)
            self.nload += 1
        b = self.blocks[bi]
        if b[0] == "in" and b[2] == "kv":
            v = self._slot(c0)[:, :].rearrange("p (k c) -> p k c", c=256)
            return lambda kt, c, n=128: v[:, kt, c:c + n]
        if b[0] == "wo":
            v = self._slot(c0)[:, :].rearrange("p (k c) -> p k c", c=512)
            return lambda kt, c, n=128: v[:, kt, c:c + n]
        va = self._slot(c0)[:, :].rearrange("p (k c) -> p k c", c=256)
        vb = self._slot(c0 + 1)[:, :].rearrange("p (k c) -> p k c", c=256)
        return lambda kt, c, n=128: (va[:, kt, c:c + n] if c < 256 else vb[:, kt, c - 256:c - 256 + n])


def build_program(nlayers=2, taps=(), stop_after=None):
    nc = bass.Bass("TRN2", target_bir_lowering=False)
    p = Prog(nc)

    def din(name, shape, dt=F32):
        return nc.dram_tensor(name, list(shape), dt, kind="ExternalInput").ap()

    def dout(name, shape, dt=F32):
        return nc.dram_tensor(name, list(shape), dt, kind="ExternalOutput").ap()

    xT_d = din("xT", [128, 16, 1024])
    cond_d = din("condT", [128, 16])
    ckT_d = din("ckT", [2, 2, 128, 512])
    cv_d = din("cv", [2, 2, 512, 128])
    sC_d = din("sC", [2, 2, 128, 4, 128])
    sN_d = din("sN", [2, 2, 128, 4, 1])
    sM_d = din("sM", [2, 64, 1])
    w_in_d = din("w_in", [2, 2048, 6160])
    w_out_d = din("w_out", [2, 2048, 2048])
    w_mod_d = din("w_mod", [2, 2048, 6144])
    normw_d = din("normw", [2, 128, 16])
    bmod_d = din("bmod", [2, 128, 48])
    bI_d = din("bI", [2, 64, 1])
    bF_d = din("bF", [2, 64, 1])
    convw_d = din("convw", [2, 128, 3, 8])
    convb_d = din("convb", [2, 128, 8])
    qnw_d = din("qnw", [2, 128, 1])
    knw_d = din("knw", [2, 128, 1])
    mlnw_d = din("mlnw", [2, 128, 4])
    wfno_d = din("wfno", [2, 4, 128, 128])
    ropeC_d = din("ropeC", [128, 1024], BF16)
    ropeS_d = din("ropeS", [128, 1024], BF16)
    mb_d = din("mb", [128, 48])
    m01_d = din("m01", [128, 48])
    rneg_d = din("rneg", [64, NCH])
    zfl_d = din("zfl", [64, NCH])
    keep_d = din("keep", [64, NCH])
    negbnd_d = din("negbnd", [128, 1])
    scan0_d = din("scan0", [64, 1024], BF16)
    tri_d = din("tri", [LCH, 2, 1, LCH])
    mask8_d = din("mask8", [64, 8, 1])
    tblT_d = din("tblT", [2, 1024, 1024], BF16)
    tblC_d = din("tblC", [128, 256], BF16)
    perm_d = din("perm", [128, 128], BF16)

    yT_o = dout("yT", [128, 16, 1024])
    kT_o = dout("kTo", [2, 2, 128, 1024])
    v_o = dout("vo", [2, 2, 1024, 128])
    C_o = dout("Co", [2, 2, 4, 128, 4, 128])
    N_o = dout("No", [2, 2, 4, 128, 4, 1])
    M_o = dout("Mo", [2, 64, NCH])

    tap_outs = {}

    def tap(name, ap):
        if name not in taps:
            return
        shp = [int(x) for x in ap.shape]
        d = dout("tap_" + name, shp, ap.dtype)
        p.dma("sp", d, ap, sem="tap_" + name)

    xT = p.sbuf("xT_s", [128, 16, 1024], F32)
    hT = p.sbuf("hT", [128, 16, 1024], BF16)
    omlT = p.sbuf("omlT", [128, 4, 1024], BF16)
    ws = WStream(p, make_plan(nlayers), w_in_d, w_out_d, w_mod_d)
    identf = p.sbuf("identf", [128, 128], F32)
    identb = p.sbuf("identb", [128, 128], BF16)
    ones_bf = p.sbuf("ones_bf", [128, 128], BF16)
    twos_bf = p.sbuf("twos_bf", [128, 128], BF16)
    ones_f = p.sbuf("ones_f", [64, 128], F32)
    perm = p.sbuf("perm_s", [128, 128], BF16)
    tblC = p.sbuf("tblC_s", [128, 256], BF16)
    mb = p.sbuf("mb_s", [128, 48], F32)
    m01 = p.sbuf("m01_s", [128, 48, 1], BF16)
    m01f = p.sbuf("m01f_s", [128, 48], F32)
    condT = p.sbuf("condT_s", [128, 16], F32)
    sc = p.sbuf("sc", [128, 16], BF16)
    modv = [p.sbuf("modv%d" % l, [128, 48], F32) for l in range(2)]
    normw = p.sbuf("normw_s", [128, 2, 16], F32)
    bmod = p.sbuf("bmod_s", [128, 2, 48], F32)
    acol = p.sbuf("acol", [128, 16], F32)
    convw = p.sbuf("convw_s", [128, 2, 3, 8], F32)
    convb = p.sbuf("convb_s", [128, 2, 8], F32)
    hw = p.sbuf("hw", [128, 2, 3, 8], F32)
    hb = p.sbuf("hb", [128, 2, 8], F32)
    nb0 = p.sbuf("nb0", [128, 2, 8], F32)
    nb2 = p.sbuf("nb2", [128, 2, 8], F32)
    negbnd = p.sbuf("negbnd_s", [128, 1], F32)
    qnw = p.sbuf("qnw_s", [128, 2, 1], F32)
    knw = p.sbuf("knw_s", [128, 2, 1], F32)
    mlnw = p.sbuf("mlnw_s", [128, 2, 4], F32)
    mlnw4 = p.sbuf("mlnw4", [128, 2, 4], F32)
    bI = p.sbuf("bI_s", [64, 2, 1], F32)
    bF = p.sbuf("bF_s", [64, 2, 1], F32)
    sM = p.sbuf("sM_s", [64, 2, 1], F32)
    rneg = p.sbuf("rneg_s", [64, NCH], F32)
    zfl = p.sbuf("zfl_s", [64, NCH], F32)
    keep = p.sbuf("keep_s", [64, NCH], F32)
    tri = p.sbuf("tri_s", [LCH, 2, 1, LCH], F32)
    mask8 = p.sbuf("mask8_s", [64, 8, 1], F32)
    wfno = p.sbuf("wfno_s", [128, 4, 128], BF16)

    PD = [p.psum("pd%d" % i, (128, 1024)) for i in range(4)]
    PS = [PD[i // 2][:, (i % 2) * 512:(i % 2 + 1) * 512] for i in range(8)]
    PSb = [PS[i][:, :].bitcast(BF16) for i in range(8)]

    H = [slice(0, 512), slice(512, 1024)]

    for q4 in range(4):
        p.dma("sp", xT[:, 4 * q4:4 * q4 + 4, :], xT_d[:, 4 * q4:4 * q4 + 4, :])
    cl = lambda dst, src: p.dma("sp", dst, src, sem="consts")
    cl(condT[:], cond_d)
    for l in range(2):
        cl(normw[:, l, :], normw_d[l])
        cl(bmod[:, l, :], bmod_d[l])
        cl(convw[:, l], convw_d[l])
        cl(convb[:, l, :], convb_d[l])
        cl(qnw[:, l, :], qnw_d[l])
        cl(knw[:, l, :], knw_d[l])
        cl(mlnw[:, l, :], mlnw_d[l])
        cl(bI[:, l, :], bI_d[l])
        cl(bF[:, l, :], bF_d[l])
        cl(sM[:, l, :], sM_d[l])
    cl(perm[:], perm_d)
    cl(tblC[:], tblC_d)
    cl(mb[:], mb_d)
    cl(m01f[:], m01_d)
    cl(negbnd[:], negbnd_d)
    cl(rneg[:], rneg_d)
    cl(zfl[:], zfl_d)
    cl(keep[:], keep_d)
    cl(tri[:], tri_d)
    cl(mask8[:], mask8_d)

    p.make_identity(identf)
    p.dve.tensor_copy(m01[:].rearrange("p a o -> p (a o)"), m01f[:])
    p.dve.tensor_copy(identb[:], identf[:])
    p.pool.memset(ones_bf[:], 1.0)
    p.pool.memset(twos_bf[:], 2.0)
    p.pool.memset(ones_f[:], 1.0)
    eps_col = p.const_col(EPS)
    one_col = p.const_col(1.0)

    p.dve.tensor_scalar(hw[:], convw[:], 0.5, None, ALU.mult)
    p.dve.tensor_scalar(hb[:], convb[:], 0.5, None, ALU.mult)
    p.dve.tensor_scalar(nb0[:], hw[:, :, 0, :], negbnd[:, 0:1], None, ALU.mult)
    p.dve.tensor_scalar(nb2[:], hw[:, :, 2, :], negbnd[:, 0:1], None, ALU.mult)
    p.dve.tensor_scalar(mlnw4[:], mlnw[:], 0.25, None, ALU.mult)
    m0 = p.scope_begin()
    tcond = p.sbuf("tcond", [128, 16], F32)
    p.act.activation(tcond[:], condT[:], AF.Tanh, scale=0.5)
    p.dve.scalar_tensor_tensor(tcond[:], tcond[:], 1.0, condT[:], ALU.add, ALU.mult)
    p.dve.tensor_scalar(sc[:], tcond[:], 0.5, None, ALU.mult)
    p.scope_end(m0)

    mod_next = {}

    def emit_mod_chunk(l, ci):
        w = ws.get(("mod", l, ci))
        for j in range(4):
            mt = 4 * ci + j
            for kt in range(KT):
                p.mm(PS[6][:, 64 + mt:65 + mt], w(kt, j * 128), sc[:, kt:kt + 1],
                     start=(kt == 0), stop=(kt == KT - 1))
            if j == 1:
                ws.advance()
        p.dve.tensor_tensor(modv[l][:, 4 * ci:4 * ci + 4], PS[6][:, 64 + 4 * ci:68 + 4 * ci],
                            bmod[:, l, 4 * ci:4 * ci + 4], ALU.add)

    modq = mod_queue(nlayers)

    def maybe_mod(l=None):
        if modq:
            emit_mod_chunk(*modq.pop(0))

    def gate_col(l, ct):
        return modv[l][:, 32 + ct:33 + ct]

    def phase_norm(l, between=None):
        p.mark("norm%d" % l)
        m = p.scope_begin()
        SQ = [p.sbuf("nsq%d" % i, [128, 1024], BF16) for i in range(2)]
        rstd = p.sbuf("nrstd", [128, 1024], F32)
        TMP = [p.sbuf("ntmp%d" % i, [128, 1024], F32) for i in range(2)]
        for kt in range(KT):
            sq = SQ[kt % 2]
            p.act.activation(sq[:], xT[:, kt, :], AF.Square)
            for hf in range(2):
                p.mm(PS[hf][:, :], ones_bf[:], sq[:, H[hf]], start=(kt == 0), stop=(kt == KT - 1))
        for hf in range(2):
            p.act.activation(rstd[:, H[hf]], PS[hf][:, :], AF.Ln, bias=eps_col, scale=1.0 / 2048)
        p.act.activation(rstd[:], rstd[:], AF.Exp, scale=-0.5)
        if between is not None:
            between()
        p.dve.scalar_tensor_tensor(acol[:], modv[l][:, 16:32], 1.0, normw[:, l, :], ALU.add, ALU.mult)
        for kt in range(KT):
            tmp = TMP[kt % 2]
            p.dve.tensor_tensor(tmp[:], xT[:, kt, :], rstd[:], ALU.mult)
            p.act.activation(hT[:, kt, :], tmp[:], AF.Identity, bias=modv[l][:, kt:kt + 1],
                             scale=acol[:, kt:kt + 1])
        tap("hT%d" % l, hT[:])
        p.scope_end(m)

    def proj_tile(w, c0, banks):
        for hf in range(2):
            ps = PS[banks[hf]]
            for kt in range(KT):
                p.mm(ps[:, :], w(kt, c0), hT[:, kt, H[hf]], start=(kt == 0), stop=(kt == KT - 1))
        return PS[banks[0]], PS[banks[1]]

    def wout_pass(l, part, rhs_fn):
        for cc in range(4):
            w = ws.get(("wo", l, part, cc))
            for ci in range(4):
                ct = cc * 4 + ci
                for hf in range(2):
                    ps = PS[(ci * 2 + hf) % 4]
                    for k in range(8):
                        p.mm(ps[:, :], w(k, ci * 128), rhs_fn(k, hf),
                             start=(k == 0), stop=(k == 7))
                    p.dve.scalar_tensor_tensor(xT[:, ct, H[hf]], ps[:, :], gate_col(l, ct),
                                               xT[:, ct, H[hf]], ALU.mult, ALU.add)
            if part == 1 and l == nlayers - 1 and stop_after is None:
                p.dma("sp", yT_o[:, cc * 4:(cc + 1) * 4, :], xT[:, cc * 4:(cc + 1) * 4, :], sem="yout")

    def attention_phase(l):
        m = p.scope_begin()
        GO = p.sbuf("GO", [128, 8, 1024], BF16)
        KT_ = p.sbuf("KT_", [128, 1536], BF16)
        Vt = p.sbuf("Vt", [128, 12, 128], BF16)
        QT = p.sbuf("QT", [128, 4, 1024], BF16)
        ropeC = p.sbuf("ropeC_s", [128, 1024], BF16)
        ropeS = p.sbuf("ropeS_s", [128, 1024], BF16)
        p.dma("sp", ropeC[:], ropeC_d)
        p.dma("sp", ropeS[:], ropeS_d)
        PSn = (PS[6], PS[7])

        for g in range(2):
            ma = p.scope_begin()
            sqb = p.sbuf("sqb", [128, 1024], BF16)
            rs = p.sbuf("rs", [128, 1024], F32)
            qn = [p.sbuf("qn%d" % a, [128, 1024], BF16) for a in range(2)]
            t1 = p.sbuf("t1", [128, 1024], F32)
            t2 = p.sbuf("t2", [128, 1024], F32)
            kf = p.sbuf("kf", [128, 1024], F32)
            vst = p.sbuf("vst", [128, 512], F32)
            PDn = PD[3]

            def qk_partA(psq):
                p.act.activation(sqb[:], psq[:, :], AF.Square)
                for hf in range(2):
                    p.mm(PSn[hf][:, :], ones_bf[:], sqb[:, H[hf]])

            def qk_partB(psq, wcol, par):
                p.act.activation(rs[:], PDn[:, :], AF.Ln, bias=eps_col, scale=1.0 / 128)
                p.act.activation(rs[:], rs[:], AF.Exp, scale=-0.5)
                p.dve.scalar_tensor_tensor(qn[par][:], psq[:, :], wcol, rs[:], ALU.mult, ALU.mult)

            def qk_part2(par, dst, f32out=None):
                for hf in range(2):
                    p.mm(PSn[hf][:, :], perm[:], qn[par][:, H[hf]])
                p.dve.tensor_tensor(t1[:], qn[par][:], ropeC[:, :], ALU.mult)
                p.dve.tensor_tensor(t2[:], PDn[:, :], ropeS[:, :], ALU.mult)
                if f32out is None:
                    p.dve.tensor_tensor(dst, t1[:], t2[:], ALU.add)
                else:
                    p.dve.tensor_tensor(f32out, t1[:], t2[:], ALU.add)
                    p.act.copy(dst, f32out)

            w = ws.get(("in", l, "kv", g))
            p.dma("pool", KT_[:, 0:512], ckT_d[l, g])
            p.dma("pool", Vt[:, 0:4, :], cv_d[l, g].rearrange("(k p) d -> p k d", p=128))
            proj_tile(w, 0, (0, 1))
            qk_partA(PD[0])
            for tt in range(8):
                ps = PS[2 + tt // 4]
                c0 = (tt % 4) * 128
                for kt in range(KT):
                    p.mm(ps[:, c0:c0 + 128], hT[:, kt, tt * 128:(tt + 1) * 128], w(kt, 128),
                         start=(kt == 0), stop=(kt == KT - 1))
            qk_partB(PD[0], knw[:, l, :], 0)
            for b in range(2):
                p.act.copy(vst[:], PS[2 + b][:, :])
                p.dma("sp", v_o[l, g, b * 512:(b + 1) * 512, :].rearrange("(k p) d -> p k d", p=128),
                      vst[:].rearrange("p (k d) -> p k d", d=128))
                p.dve.tensor_copy(Vt[:, 4 + 4 * b:8 + 4 * b, :], vst[:].rearrange("p (k d) -> p k d", d=128))
            maybe_mod(l)
            w = ws.get(("in", l, "aq", g))
            bank = lambda h: ((0, 1) if h % 2 == 0 else (2, 3))
            proj_tile(w, 0, bank(0))
            qk_part2(0, KT_[:, 512:1536], f32out=kf[:])
            p.dma("sp", kT_o[l, g], kf[:])
            for h in range(4):
                qk_partA(PD[h % 2])
                if h + 1 < 4:
                    proj_tile(w, (h + 1) * 128, bank(h + 1))
                    if h + 1 == 1:
                        ws.advance()
                qk_partB(PD[h % 2], qnw[:, l, :], h % 2)
                qk_part2(h % 2, QT[:, h, :])
            maybe_mod(l)
            w = ws.get(("in", l, "ag", g))
            for h in range(4):
                proj_tile(w, h * 128, ((0, 1) if h % 2 == 0 else (2, 3)))
                psg = PD[h % 2]
                if h == 1:
                    ws.advance()
                p.act.activation(kf[:], psg[:, :], AF.Tanh, scale=0.5)
                p.dve.scalar_tensor_tensor(GO[:, 4 * g + h, :], kf[:], 1.0, psg[:, :], ALU.add, ALU.mult)
            maybe_mod(l)
            p.scope_end(ma)
            if g == 0:
                tap("KT%d" % l, KT_[:])
                tap("QT%d" % l, QT[:])
                tap("Vt%d" % l, Vt[:])
            p.mark("att%d.g%d.loop" % (l, g))
            mb_ = p.scope_begin()
            PT = [p.sbuf("PT%d" % i, [128, 1024], BF16) for i in range(4)]
            rL = [[p.sbuf("rL%d%d" % (a, i), [128, 512], F32) for i in range(2)] for a in range(2)]
            oS = [[p.sbuf("oS%d%d" % (a, i), [128, 512], F32) for i in range(2)] for a in range(2)]
            steps = [(h, kt) for h in range(4) for kt in range(12)]
            Sb = (PD[1], PD[3])
            psO = (PS[0], PS[1])
            psL = (PS[4], PS[5])

            def emit_S(i):
                h, kt = steps[i]
                for qc in range(2):
                    p.mm(Sb[i % 2][:, H[qc]], KT_[:, kt * 128:(kt + 1) * 128], QT[:, h, H[qc]], inc=(qc == 1))

            def emit_E(i):
                h, kt = steps[i]
                pt = PT[i % 4]
                if kt < 4:
                    p.act.activation(pt[:], Sb[i % 2][:, :], AF.Exp, bias=mb[:, kt * 4:kt * 4 + 1], scale=ATT_SCALE)
                else:
                    p.act.activation(pt[:], Sb[i % 2][:, :], AF.Exp, scale=ATT_SCALE)
                    for qb in range(4):
                        blk = pt[:, qb * 256:(qb + 1) * 256]
                        p.dve.tensor_scalar(blk, blk, m01f[:, kt * 4 + qb:kt * 4 + qb + 1], None, ALU.mult)

            def emit_PV(i):
                h, kt = steps[i]
                pt = PT[i % 4]
                for qc in range(2):
                    p.mm(psO[qc][:, :], Vt[:, kt, :], pt[:, H[qc]], start=(kt == 0), stop=(kt == 11))
                for qc in range(2):
                    p.mm(psL[qc][:, :], twos_bf[:], pt[:, H[qc]], start=(kt == 0), stop=(kt == 11))
                if kt == 11:
                    hb = h % 2
                    for qc in range(2):
                        p.act.activation(rL[hb][qc][:], psL[qc][:, :], AF.Ln)
                        p.act.activation(rL[hb][qc][:], rL[hb][qc][:], AF.Exp, scale=-1.0)
                        p.dve.tensor_copy(oS[hb][qc][:], psO[qc][:, :])
                    for qc in range(2):
                        p.pool.tensor_tensor(rL[hb][qc][:], rL[hb][qc][:], GO[:, 4 * g + h, H[qc]], ALU.mult)

                    def fin(h=h, hb=hb):
                        for qc in range(2):
                            p.dve.tensor_tensor(GO[:, 4 * g + h, H[qc]], oS[hb][qc][:], rL[hb][qc][:], ALU.mult)
                    pending.append([3, fin])

            pending = []
            emit_S(0)
            for i in range(len(steps)):
                if i + 1 < len(steps):
                    emit_S(i + 1)
                emit_E(i)
                for pe_ in list(pending):
                    pe_[0] -= 1
                    if pe_[0] <= 0:
                        pe_[1]()
                        pending.remove(pe_)
                if i >= 1:
                    emit_PV(i - 1)
            emit_PV(len(steps) - 1)
            for pe_ in pending:
                pe_[1]()
            p.scope_end(mb_)
        p.mark("att%d.wout" % l)
        tap("oatt%d" % l, GO[:])
        wout_pass(l, 0, lambda k, hf: GO[:, k, H[hf]])
        tap("xatt%d" % l, xT[:])
        p.scope_end(m)

    def mlstm_phase(l):
        p.mark("ml%d.qkconv" % l)
        mo_ = p.scope_begin()
        SCR = p.sbuf("SCR", [128, 4096], F32)
        T = [SCR[:, i * 1024:(i + 1) * 1024] for i in range(4)]
        Hs = SCR[0:LCH, 0:NCH * 256].bitcast(BF16).rearrange("p (c h d) -> p c h d", h=4, d=128)
        mi = p.scope_begin()
        qT = p.sbuf("mqT", [128, 4, 1024], BF16)
        ksT = p.sbuf("mksT", [128, 4, 1024], BF16)
        Vtok = p.sbuf("Vtok", [LCH, NCH, 4, 128], BF16)
        cwtok = p.sbuf("cwtok", [LCH, NCH, 8, 1], F32)
        ebtok = p.sbuf("ebtok", [LCH, NCH, 8, 1], F32)
        cwb = p.sbuf("cwb", [LCH, NCH, 8, 1], BF16)
        decbc = p.sbuf("decbc", [128, 8, NCH], F32)

        def conv_tile(ct, psq, dst):
            X, acc, tt_ = T[0], T[1], T[2]
            p.act.copy(X, psq[:, :])
            p.dve.tensor_scalar(acc, X, hw[:, l, 1, ct:ct + 1], hb[:, l, ct:ct + 1], ALU.mult, ALU.add)
            p.dve.scalar_tensor_tensor(acc[:, 1:1024], X[:, 0:1023], hw[:, l, 0, ct:ct + 1], acc[:, 1:1024],
                                       ALU.mult, ALU.add)
            p.dve.scalar_tensor_tensor(acc[:, 0:1023], X[:, 1:1024], hw[:, l, 2, ct:ct + 1], acc[:, 0:1023],
                                       ALU.mult, ALU.add)
            p.dve.scalar_tensor_tensor(acc[:, 256:1024:256], X[:, 255:1023:256], nb0[:, l, ct:ct + 1],
                                       acc[:, 256:1024:256], ALU.mult, ALU.add)
            p.dve.scalar_tensor_tensor(acc[:, 255:1023:256], X[:, 256:1024:256], nb2[:, l, ct:ct + 1],
                                       acc[:, 255:1023:256], ALU.mult, ALU.add)
            p.act.activation(tt_, acc, AF.Tanh)
            p.dve.scalar_tensor_tensor(dst, tt_, 1.0, acc, ALU.add, ALU.mult)

        for (nm, dstT, cto) in (("mq", qT, 0), ("mk", ksT, 4)):
            w = ws.get(("in", l, nm, 0))
            for h in range(4):
                pss = proj_tile(w, h * 128, ((0, 1) if h % 2 == 0 else (2, 3)))
                if h == 1:
                    ws.advance()
                conv_tile(cto + h, PD[h % 2], dstT[:, h, :])
            maybe_mod(l)
        tap("mqT%d" % l, qT[:])
        tap("mksT%d" % l, ksT[:])

        p.mark("ml%d.gates" % l)
        mg_ = p.scope_begin()
        tot = p.sbuf("g_tot", [64, NCH], F32)
        amax = p.sbuf("g_amax", [64, NCH], F32)
        d0 = p.sbuf("g_d0", [64, NCH], F32)
        d1 = p.sbuf("g_d1", [64, NCH], F32)
        mnext = p.sbuf("g_mnext", [64, NCH], F32)
        min_ = p.sbuf("g_min", [64, NCH], F32)
        Ag = p.sbuf("g_A", [64, NCH, 1], F32)
        dec = p.sbuf("g_dec", [64, NCH], F32)
        decn = p.sbuf("g_decn", [64, 1, NCH], F32)
        Eg = p.sbuf("g_E", [64, 8, NCH], F32)
        scan0 = p.sbuf("scan0_s", [64, 1024], BF16)
        WgI = p.sbuf("WgI", [128, 16, 64], BF16)
        WgF = p.sbuf("WgF", [128, 16, 64], BF16)
        Wg16 = p.sbuf("Wg16", [128, 16, 16], F32)
        p.dma("sp", scan0[:], scan0_d)
        p.pool.memset(WgI[:], 0.0)
        p.pool.memset(WgF[:], 0.0)
        p.dma("sp", Wg16[:], w_in_d[l][:, C_MIF:C_MIF + 16].rearrange("(k p) c -> p k c", p=128))
        p.pool.tensor_copy(WgI[:, :, 0:4], Wg16[:, :, 0:4])
        p.pool.tensor_copy(WgI[:, :, 32:36], Wg16[:, :, 8:12])
        p.pool.tensor_copy(WgF[:, :, 0:4], Wg16[:, :, 4:8])
        p.pool.tensor_copy(WgF[:, :, 32:36], Wg16[:, :, 12:16])
        for hf in range(2):
            for kt in range(KT):
                p.mm(PS[hf][0:64, :], WgI[:, kt, :], hT[:, kt, H[hf]], start=(kt == 0), stop=(kt == KT - 1))
            for kt in range(KT):
                p.mm(PS[2 + hf][0:64, :], WgF[:, kt, :], hT[:, kt, H[hf]], start=(kt == 0), stop=(kt == KT - 1))
        T0, T1, T2, T3 = [t[0:64, :] for t in T]
        for hf in range(2):
            p.act.activation(T0[:, H[hf]], PS[hf][0:64, :], AF.Identity, bias=bI[:, l, :])
            p.act.activation(T1[:, H[hf]], PS[2 + hf][0:64, :], AF.Identity, bias=bF[:, l, :])
        p.defer_begin()
        p.act.activation(T1, T1, AF.Exp, scale=-1.0)
        p.act.activation(T1, T1, AF.Ln, bias=one_col[0:64, :], scale=1.0)
        for Tx in (T[0], T[1]):
            p.dve.tensor_copy(T[2][32:36, :], Tx[32:36, ::-1])
            p.dve.tensor_copy(Tx[32:36, :], T[2][32:36, :])
        p.dve.tensor_tensor_scan(T3, scan0[:, :], T1, 0.0, ALU.mult, ALU.add)
        p.dve.tensor_tensor(T0, T0, T3, ALU.add)
        c3 = lambda ap: ap.rearrange("p (c l) -> p c l", l=LCH)
        p.dve.tensor_copy(tot[:], c3(T3)[:, :, LCH - 1])
        p.dve.tensor_reduce(amax[:], c3(T0), AX.X, ALU.max)
        p.dve.tensor_tensor(d0[:], rneg[:], tot[:], ALU.subtract)
        p.dve.tensor_tensor(d1[:], amax[:], zfl[:], ALU.max)
        p.dve.tensor_tensor(d1[:], d1[:], tot[:], ALU.subtract)
        p.dve.tensor_tensor_scan(mnext[:], d0[:], d1[:], sM[:, l, :], ALU.add, ALU.max)
        p.dma("sp", M_o[l], mnext[:])
        p.dve.tensor_copy(min_[:, 0:1], sM[:, l, :])
        p.dve.tensor_copy(min_[:, 1:NCH], mnext[:, 0:NCH - 1])
        p.dve.tensor_tensor(min_[:], min_[:], keep[:], ALU.mult)
        A2 = Ag[:].rearrange("p c o -> p (c o)")
        p.dve.tensor_tensor(A2, min_[:], amax[:], ALU.max)
        p.dve.tensor_tensor(dec[:], min_[:], A2, ALU.subtract)
        p.act.activation(dec[:], dec[:], AF.Exp)
        p.dve.tensor_tensor(dec[:], dec[:], keep[:], ALU.mult)
        Abc = Ag[:].broadcast_to([64, NCH, LCH])
        p.dve.tensor_tensor(c3(T0), c3(T0), Abc, ALU.subtract)
        p.act.activation(T0, T0, AF.Exp)
        p.dve.tensor_tensor(c3(T3), c3(T3), Abc, ALU.subtract)
        p.act.activation(T3, T3, AF.Exp)
        for Tx in (T[0], T[3]):
            p.dve.tensor_copy(T[2][32:36, :], Tx[32:36, ::-1])
            p.dve.tensor_copy(Tx[32:36, :], T[2][32:36, :])
        dn2 = decn[:].rearrange("p o c -> p (o c)")
        p.dve.tensor_copy(dn2, dec[:])
        p.dve.tensor_copy(dn2[32:36, :], dec[32:36, ::-1])
        gate_chain = p.defer_end()
        p.mark("ml%d.vproj" % l)
        w = ws.get(("in", l, "mv", 0))
        for j in range(NCH):
            ps = PS[2 + j % 2]
            for cc in range(2):
                for kt in range(KT):
                    p.mm(ps[0:LCH, cc * 256:(cc + 1) * 256], hT[:, kt, j * LCH:(j + 1) * LCH], w(kt, cc * 256, 256),
                         start=(kt == 0), stop=(kt == KT - 1))
            p.act.copy(Vtok[:, j, :, :], ps[0:LCH, :].rearrange("t (h d) -> t h d", d=128))
            p.replay(gate_chain, 7)
        p.replay(gate_chain, 10 ** 6)
        maybe_mod(l)
        for (Tx, dst) in ((T0, cwtok), (T3, ebtok)):
            for b in range(NCH // 8):
                ps = PS[4 + b]
                for c8 in range(8):
                    c = b * 8 + c8
                    p.transpose(ps[0:LCH, c8 * 64:(c8 + 1) * 64], Tx[:, c * LCH:(c + 1) * LCH], identf[0:64, 0:64],
                                inc=(c8 == 7))
                psv = ps[0:LCH, :].rearrange("t (c q) -> t c q", q=64)
                d3 = dst[:].rearrange("t c q o -> t c (q o)")
                p.dve.tensor_copy(d3[:, b * 8:(b + 1) * 8, 0:4], psv[:, :, 0:4])
                p.dve.tensor_copy(d3[:, b * 8:(b + 1) * 8, 4:8], psv[:, :, 32:36])
        p.dve.tensor_copy(cwb[:], cwtok[:])
        p.dve.tensor_tensor(Eg[:], mask8[:].broadcast_to([64, 8, NCH]), decn[:].broadcast_to([64, 8, NCH]), ALU.mult)
        p.mm(PS[6][:, 0:8 * NCH], ones_f[:, :], Eg[:].rearrange("p a b -> p (a b)"))
        p.act.copy(decbc[:].rearrange("p a b -> p (a b)"), PS[6][:, 0:8 * NCH])
        p.scope_end(mg_)

        p.mark("ml%d.chains" % l)
        CN = [[p.sbuf("CN%d%d" % (d, i), [128, 4, 129], F32) for i in range(2)] for d in range(2)]
        CNb = [p.sbuf("CNb%d" % d, [128, 4, 129], BF16) for d in range(2)]
        VP = [[p.sbuf("VP%d%d" % (d, i), [LCH, 4, 128], BF16) for i in range(2)] for d in range(2)]
        SM = [[p.sbuf("SM%d%d" % (d, i), [LCH, 4, LCH], BF16) for i in range(2)] for d in range(2)]
        KTOK = [[p.sbuf("KTOK%d" % d, [LCH, 4, 128], BF16)] * 2 for d in range(2)]
        dn = [p.sbuf("g_dn%d" % d, [LCH, 4, 1], F32) for d in range(2)]
        hbuf = [p.sbuf("g_hb%d" % d, [LCH, 4, 128], BF16) for d in range(2)]
        for dr in range(2):
            p.dma("sp", CN[dr][0][:, :, 0:128], sC_d[l, dr])
            p.dma("sp", CN[dr][0][:, :, 128:129], sN_d[l, dr], slow=True)
        psS = (PS[0], PS[1])
        psN = (PS[2], PS[3])
        psU = (PS[4], PS[5])
        jof = lambda dr, i: (i if dr == 0 else NCH - 1 - i)

        def stage_A1(dr, i):
            j = jof(dr, i)
            cs = slice(j * LCH, (j + 1) * LCH)
            vp, sm, kt_ = VP[dr][i % 2], SM[dr][i % 2], KTOK[dr][i % 2]
            p.pool.tensor_tensor(vp[:], Vtok[:, j, :, :],
                                 cwtok[:, j, dr * 4:(dr + 1) * 4, :].broadcast_to([LCH, 4, 128]), ALU.mult)
            for h in range(4):
                p.mm(psS[dr][0:LCH, h * LCH:(h + 1) * LCH], ksT[:, h, cs], qT[:, h, cs], inc=(h == 3))
            p.dve.scalar_tensor_tensor(sm[:], psS[dr][0:LCH, 0:4 * LCH].rearrange("s (h l) -> s h l", l=LCH), MLK_SCALE,
                                       tri[:, dr, :, :].broadcast_to([LCH, 4, LCH]), ALU.mult, ALU.mult)
            psT = PSb[7]
            o = dr * 512
            for h in range(4):
                p.transpose(psT[0:LCH, o + h * 128:o + (h + 1) * 128], ksT[:, h, cs], identb[:, :], inc=(h == 3))
            p.act.mul(kt_[:].rearrange("s h d -> s (h d)"), psT[0:LCH, o:o + 512], MLK_SCALE)

        def stage_A2(dr, i):
            j = jof(dr, i)
            vp, kt_ = VP[dr][i % 2], KTOK[dr][i % 2]
            for h in range(4):
                p.mm(psU[dr][:, h * 128:(h + 1) * 128], kt_[:, h, :], vp[:, h, :], inc=(h == 3))
            co = 16 + 8 * dr
            for h in range(4):
                p.mm(PS[6][:, co + h:co + h + 1], kt_[:, h, :], cwb[:, j, dr * 4 + h, :], inc=(h == 3))

        def stage_B(dr, i):
            j = jof(dr, i)
            cs = slice(j * LCH, (j + 1) * LCH)
            vp, sm = VP[dr][i % 2], SM[dr][i % 2]
            cur, cn, cnb = CN[dr][i % 2], CN[dr][(i + 1) % 2], CNb[dr]
            co = 16 + 8 * dr
            cd = 8 * dr
            for h in range(4):
                p.act.activation(cnb[:, h, :], cur[:, h, :], AF.Identity, scale=decbc[:, dr * 4 + h, j:j + 1])
            for h in range(4):
                p.mm(psN[dr][0:LCH, h * 128:(h + 1) * 128], qT[:, h, cs], cnb[:, h, 0:128], start=True, stop=False,
                     inc=False)
                p.mm(psN[dr][0:LCH, h * 128:(h + 1) * 128], sm[:, h, :], vp[:, h, :], start=False, stop=True,
                     inc=(h == 3))
            for h in range(4):
                p.mm(PS[6][0:LCH, cd + h:cd + h + 1], qT[:, h, cs], cnb[:, h, 128:129], start=True, stop=False,
                     inc=False)
                p.mm(PS[6][0:LCH, cd + h:cd + h + 1], sm[:, h, :], cwb[:, j, dr * 4 + h, :], start=False,
                     stop=True, inc=(h == 3))
            for h in range(4):
                p.dve.scalar_tensor_tensor(cn[:, h, 0:128], cur[:, h, 0:128], decbc[:, dr * 4 + h, j:j + 1],
                                           psU[dr][:, h * 128:(h + 1) * 128], ALU.mult, ALU.add)
            p.dve.tensor_tensor(cn[:, :, 128:129], cur[:, :, 128:129], decbc[:, dr * 4:(dr + 1) * 4, j:j + 1],
                                ALU.mult)
            p.dve.tensor_tensor(cn[:, :, 128:129], cn[:, :, 128:129],
                                PS[6][:, co:co + 4].rearrange("p (h o) -> p h o", o=1), ALU.add)
            den = PS[6][0:LCH, cd:cd + 4].rearrange("p (h o) -> p h o", o=1)
            d_ = dn[dr]
            p.dve.tensor_tensor(d_[:], den, ebtok[:, j, dr * 4:(dr + 1) * 4, :], ALU.max)
            p.dve.scalar_tensor_tensor(d_[:], den, -1.0, d_[:], ALU.mult, ALU.max)
            p.dve.reciprocal(d_[:], d_[:])
            num3 = psN[dr][0:LCH, :].rearrange("p (h d) -> p h d", d=128)
            if i < NCH // 2:
                p.dve.tensor_tensor(Hs[:, j], num3, d_[:].broadcast_to([LCH, 4, 128]), ALU.mult)
            else:
                p.dve.tensor_tensor(hbuf[dr][:], num3, d_[:].broadcast_to([LCH, 4, 128]), ALU.mult)
                p.pool.tensor_tensor(Hs[:, j], Hs[:, j], hbuf[dr][:], ALU.add)
            cps = 256 // LCH
            if (dr == 0 and j % cps == cps - 1) or (dr == 1 and j % cps == 0):
                s_ = j // cps
                p.dma("sp", C_o[l, dr, s_], cn[:, :, 0:128], sem="CNo%d" % dr)
                p.dma("sp", N_o[l, dr, s_], cn[:, :, 128:129], sem="CNo%d" % dr, slow=True)

        for dr in range(2):
            stage_A1(dr, 0)
            stage_A2(dr, 0)
        for i in range(NCH):
            for dr in range(2):
                if i + 1 < NCH:
                    stage_A1(dr, i + 1)
                stage_B(dr, i)
                if i + 1 < NCH:
                    stage_A2(dr, i + 1)
            if i % (NCH // 4) == NCH // 4 - 1:
                maybe_mod()
        tap("Hs%d" % l, Hs)
        p.scope_end(mi)

        p.mark("ml%d.out" % l)
        mi = p.scope_begin()
        Gm = p.sbuf("Gm", [128, 4, 1024], BF16)
        tg = p.sbuf("mtg", [128, 512], F32)
        tu = p.sbuf("mtu", [128, 512], F32)
        ssq = p.sbuf("mssq", [LCH, NCH * 4, 1], F32)
        junk = p.sbuf("mjunk", [LCH, 128], BF16)
        ssq_jobs = [(j, h) for j in range(NCH) for h in range(4)]

        def some_ssq(n):
            for _ in range(n):
                if ssq_jobs:
                    j, h = ssq_jobs.pop(0)
                    p.act.activation(junk[:], Hs[:, j, h, :], AF.Square, accum_out=ssq[:, j * 4 + h, :])

        some_ssq(NCH * 4 // 8)
        w = ws.get(("in", l, "mo", 0))
        for h in range(4):
            pss = proj_tile(w, h * 128, (0, 1))
            if h == 1:
                ws.advance()
            for hf in range(2):
                p.act.activation(Gm[:, h, H[hf]], pss[hf][:, :], AF.Tanh, scale=0.5)
            some_ssq(NCH * 4 // 8)
        maybe_mod(l)
        w = ws.get(("in", l, "mg", 0))
        for h in range(4):
            pss = proj_tile(w, h * 128, (2, 3))
            if h == 1:
                ws.advance()
            for hf in range(2):
                p.act.activation(tg[:], pss[hf][:, :], AF.Tanh, scale=0.5)
                p.dve.scalar_tensor_tensor(tu[:], tg[:], 1.0, pss[hf][:, :], ALU.add, ALU.mult)
                p.dve.scalar_tensor_tensor(Gm[:, h, H[hf]], Gm[:, h, H[hf]], 1.0, tu[:], ALU.add, ALU.mult)
            some_ssq(NCH * 4 // 8)
        some_ssq(NCH * 4)
        ssq2 = ssq[:].rearrange("p a o -> p (a o)")
        p.act.activation(ssq2, ssq2, AF.Ln, bias=eps_col[0:LCH, :], scale=1.0 / 128)
        p.act.activation(ssq2, ssq2, AF.Exp, scale=-0.5)
        Hs3 = Hs.rearrange("p c h d -> p (c h) d")
        p.dve.tensor_tensor(Hs3, Hs3, ssq[:].broadcast_to([LCH, NCH * 4, 128]), ALU.mult)
        maybe_mod(l)
        for h in range(4):
            psT = PSb[6 + h % 2]
            for j in range(NCH):
                p.transpose(psT[:, j * LCH:(j + 1) * LCH], Hs[:, j, h, :], identb[0:LCH, 0:LCH], inc=(j == NCH - 1))
            p.dve.scalar_tensor_tensor(omlT[:, h, :], psT[:, :], mlnw4[:, l, h:h + 1], Gm[:, h, :], ALU.mult, ALU.mult)
        tap("omlT%d" % l, omlT[:])
        p.scope_end(mi)
        p.scope_end(mo_)

    def fft_phase(l):
        p.mark("fft%d.proj" % l)
        m = p.scope_begin()
        fxT = p.sbuf("fxT", [128, 4, 1024], BF16)
        F2 = p.sbuf("F2", [128, 4, 1024], BF16)
        Y = p.sbuf("Yf", [128, 8, 4, 2, 128], BF16)
        TB = [p.sbuf("TB%d" % i, [128, 2, 512], BF16) for i in range(3)]
        ZT = p.sbuf("ZT", [128, 4, 512], BF16)
        tg = p.sbuf("ftg", [128, 512], F32)
        p.dma("pool", wfno[:], wfno_d[l].rearrange("g c d -> c g d"))
        w = ws.get(("in", l, "fx", 0))
        for g in range(4):
            pss = proj_tile(w, g * 128, ((0, 1) if g % 2 == 0 else (2, 3)))
            if g == 1:
                ws.advance()
            for hf in range(2):
                p.act.copy(fxT[:, g, H[hf]], pss[hf][:, :])
        maybe_mod(l)
        w = ws.get(("in", l, "fg", 0))
        for g in range(4):
            pss = proj_tile(w, g * 128, ((0, 1) if g % 2 == 0 else (2, 3)))
            if g == 1:
                ws.advance()
            for hf in range(2):
                p.act.activation(tg[:], pss[hf][:, :], AF.Tanh, scale=0.5)
                p.dve.scalar_tensor_tensor(F2[:, g, H[hf]], tg[:], 1.0, pss[hf][:, :], ALU.add, ALU.mult)
        maybe_mod(l)
        p.mark("fft%d.dft" % l)
        for tt in range(8):
            for g in range(4):
                ps = PS[2 * (tt % 2) + g // 2]
                p.mm(ps[:, (g % 2) * 256:(g % 2) * 256 + 256], fxT[:, g, tt * 128:(tt + 1) * 128], tblC[:, :],
                     inc=(g % 2 == 1))
            for b in range(2):
                ps = PS[2 * (tt % 2) + b]
                dst = Y[:, tt, 2 * b:2 * b + 2, :, :].rearrange("p g s c -> p (g s c)")
                if b == 0:
                    p.act.copy(dst, ps[:, :])
                else:
                    p.dve.tensor_copy(dst, ps[:, :])
        for hf in range(2):
            for tt in range(8):
                tb = TB[(hf * 8 + tt) % 3]
                p.dma("sp", tb[:], tblT_d[:, tt * 128:(tt + 1) * 128, H[hf]].rearrange("s t u -> t s u"))
                for cs_ in range(2):
                    for g in range(4):
                        p.mm(PS[g][:, :], Y[:, tt, g, cs_, :], tb[:, cs_, :],
                             start=(tt == 0 and cs_ == 0), stop=(tt == 7 and cs_ == 1),
                             inc=(cs_ == 1 and g == 3))
            for g in range(4):
                p.act.mul(ZT[:, g, :], PS[g][:, :], 0.5)
                ps = PS[4 + g % 2]
                p.mm(ps[:, :], wfno[:, g, :], ZT[:, g, :])
                p.dve.tensor_tensor(F2[:, g, H[hf]], ps[:, :], F2[:, g, H[hf]], ALU.mult)
        p.mark("fft%d.wout" % l)
        tap("ofoT%d" % l, F2[:])
        wout_pass(l, 1, lambda k, hf: (omlT[:, k, H[hf]] if k < 4 else F2[:, k - 4, H[hf]]))
        p.scope_end(m)

    def first_mod():
        for ci in range(8):
            emit_mod_chunk(0, ci)

    for l in range(nlayers):
        phase_norm(l, between=(first_mod if l == 0 else None))
        if stop_after == ("norm", l):
            break
        if attention_phase(l):
            break
        if stop_after == ("att", l):
            break
        mlstm_phase(l)
        if stop_after == ("ml", l):
            break
        fft_phase(l)
        tap("xout%d" % l, xT[:])
    p.mark("end")
    if stop_after is not None:
        for q4 in range(4):
            p.dma("sp", yT_o[:, 4 * q4:4 * q4 + 4, :], xT[:, 4 * q4:4 * q4 + 4, :], sem="yout")
    p.finish()
    nc._marks = p.marks
    return nc


_bf = ml_dtypes.bfloat16
_CONST_CACHE = {}


def _consts(kind):
    if kind in _CONST_CACHE:
        return _CONST_CACHE[kind]
    prompt = kind == "prompt"
    c = {}
    t = np.arange(1024)
    rows, cols = t // 64, t % 64
    inv = 10000.0 ** (-np.arange(0, 64, 2, dtype=np.float64) / 64)
    C = np.ones((128, 1024)); S = np.zeros((128, 1024))
    perm = np.zeros((128, 128), np.float32)
    for d in range(128):
        blk, w = d // 64, d % 64
        pos = rows if blk == 0 else cols
        i = w % 32
        ang = pos * inv[i]
        first = w < 32
        partner = d + 32 if first else d - 32
        perm[partner, d] = 1.0
        if not prompt:
            C[d] = np.cos(ang)
            S[d] = -np.sin(ang) if first else np.sin(ang)
    c["ropeC"] = C.astype(_bf)
    c["ropeS"] = S.astype(_bf)
    c["perm"] = perm.astype(_bf)
    mb = np.zeros((128, 48), np.float32)
    if prompt:
        for kt in range(12):
            for qb in range(4):
                ok = kt >= 4 and (kt - 4) // 2 == qb
                mb[:, kt * 4 + qb] = 0.0 if ok else NEG
    c["mb"] = mb
    c["m01"] = (mb == 0.0).astype(np.float32)
    rneg = np.zeros((64, NCH), np.float32); zfl = np.full((64, NCH), -1e30, np.float32)
    keep = np.ones((64, NCH), np.float32)
    if prompt:
        for j in range(256 // LCH, NCH, 256 // LCH):
            rneg[:, j] = -1e30; zfl[:, j] = 0.0; keep[:, j] = 0.0
    c["rneg"], c["zfl"], c["keep"] = rneg, zfl, keep
    c["negbnd"] = np.full((128, 1), -1.0 if prompt else 0.0, np.float32)
    s0 = np.ones((64, 1024), np.float32); s0[:, ::LCH] = 0.0
    c["scan0"] = s0.astype(_bf)
    tri = np.zeros((LCH, 2, 1, LCH), np.float32)
    s_, l_ = np.meshgrid(np.arange(LCH), np.arange(LCH), indexing="ij")
    tri[:, 0, 0, :] = (s_ <= l_)
    tri[:, 1, 0, :] = (s_ >= l_)
    c["tri"] = tri
    m8 = np.zeros((64, 8, 1), np.float32)
    for i in range(8):
        m8[32 * (i // 4) + i % 4, i, 0] = 1.0
    c["mask8"] = m8
    T = 256 if prompt else 1024
    tt = np.arange(1024)
    same = (tt[:, None] // T) == (tt[None, :] // T)
    ang = 2 * np.pi * ((tt[:, None] % T) * (tt[None, :] % T) % T) / T
    nrm = 1.0 / np.sqrt(T * 128.0)
    tbl = np.stack([np.cos(ang) * nrm * same, -np.sin(ang) * nrm * same])
    c["tblT"] = tbl.astype(_bf)
    cc = np.arange(128)
    angc = 2 * np.pi * ((cc[:, None] * cc[None, :]) % 128) / 128
    c["tblC"] = np.concatenate([np.cos(angc), np.sin(angc)], axis=1).astype(_bf)
    _CONST_CACHE[kind] = c
    return c


def _shared(inp):
    f = lambda a: np.ascontiguousarray(a, dtype=np.float32)
    sh = {}
    sh["w_in"] = f(inp["w_in"]); sh["w_out"] = f(inp["w_out"]); sh["w_mod"] = f(inp["w_mod"])
    sh["normw"] = f(inp["norm_w"].reshape(2, 16, 128).transpose(0, 2, 1))
    sh["bmod"] = f(inp["b_mod"].reshape(2, 48, 128).transpose(0, 2, 1))
    bif = np.asarray(inp["b_if"], np.float32)
    bI = np.zeros((2, 64, 1), np.float32); bF = np.zeros((2, 64, 1), np.float32)
    bI[:, 0:4, 0] = bif[:, 0:4]; bF[:, 0:4, 0] = bif[:, 4:8]
    bI[:, 32:36, 0] = bif[:, 8:12]; bF[:, 32:36, 0] = bif[:, 12:16]
    sh["bI"], sh["bF"] = bI, bF
    sh["convw"] = f(inp["conv_w"].reshape(2, 3, 8, 128).transpose(0, 3, 1, 2))
    sh["convb"] = f(inp["conv_b"].reshape(2, 8, 128).transpose(0, 2, 1))
    sh["qnw"] = f(inp["q_norm"].reshape(2, 128, 1))
    sh["knw"] = f(inp["k_norm"].reshape(2, 128, 1))
    sh["mlnw"] = f(inp["ml_norm"].reshape(2, 4, 128).transpose(0, 2, 1))
    sh["wfno"] = f(inp["w_fno"])
    return sh


def _core_inputs(c, inp, sh):
    f = lambda a: np.ascontiguousarray(a, dtype=np.float32)
    d = dict(sh)
    if c < 4:
        d.update(_consts("prompt"))
        x = np.asarray(inp["x_prompt"])[4 * c:4 * c + 4].reshape(1024, 2048)
        cond = np.asarray(inp["c_ctx"])
        d["ckT"] = np.zeros((2, 2, 128, 512), np.float32)
        d["cv"] = np.zeros((2, 2, 512, 128), np.float32)
        d["sC"] = np.zeros((2, 2, 128, 4, 128), np.float32)
        d["sN"] = np.zeros((2, 2, 128, 4, 1), np.float32)
        d["sM"] = np.zeros((2, 64, 1), np.float32)
    else:
        b = c - 4
        d.update(_consts("sample"))
        x = np.asarray(inp["x_sample"])[b]
        cond = np.asarray(inp["c"])[b]
        d["ckT"] = f(np.asarray(inp["cache_k"])[b].transpose(0, 2, 3, 1))
        d["cv"] = f(np.asarray(inp["cache_v"])[b].transpose(0, 2, 1, 3))
        d["sC"] = f(np.asarray(inp["state_C"])[b].transpose(0, 1, 3, 2, 4))
        d["sN"] = f(np.asarray(inp["state_n"])[b].transpose(0, 1, 3, 2)[..., None])
        sm = np.asarray(inp["state_m"])[b]
        sM = np.zeros((2, 64, 1), np.float32)
        sM[:, 0:4, 0] = sm[:, 0]; sM[:, 32:36, 0] = sm[:, 1]
        d["sM"] = sM
    d["xT"] = f(x.reshape(1024, 16, 128).transpose(2, 1, 0))
    d["condT"] = f(cond.reshape(16, 128).T)
    return d


_NC_CACHE = {}


def kernel(**inputs):
    if "nc" not in _NC_CACHE:
        _NC_CACHE["nc"] = build_program()
    nc = _NC_CACHE["nc"]
    sh = _shared(inputs)
    in_maps = [_core_inputs(c, inputs, sh) for c in range(8)]
    res = run_bass_kernel_spmd(nc, in_maps, core_ids=list(range(8)))
    R = res.results
    y_prompt = np.zeros((16, 256, 2048), np.float32)
    y_sample = np.zeros((4, 1024, 2048), np.float32)
    new_k = np.zeros((16, 2, 256, 2, 128), np.float32)
    new_v = np.zeros((16, 2, 256, 2, 128), np.float32)
    new_C = np.zeros((16, 2, 2, 4, 128, 128), np.float32)
    new_n = np.zeros((16, 2, 2, 4, 128), np.float32)
    new_m = np.zeros((16, 2, 2, 4), np.float32)
    for c in range(8):
        r = R[c]
        y = np.asarray(r["yT"]).transpose(2, 1, 0).reshape(1024, 2048)
        if c >= 4:
            y_sample[c - 4] = y
            continue
        y_prompt[4 * c:4 * c + 4] = y.reshape(4, 256, 2048)
        kT = np.asarray(r["kTo"]); vo = np.asarray(r["vo"])
        Co = np.asarray(r["Co"]); No = np.asarray(r["No"]); Mo = np.asarray(r["Mo"])
        for s in range(4):
            bi = 4 * c + s
            new_k[bi] = kT[:, :, :, s * 256:(s + 1) * 256].transpose(0, 3, 1, 2)
            new_v[bi] = vo[:, :, s * 256:(s + 1) * 256, :].transpose(0, 2, 1, 3)
            new_C[bi] = Co[:, :, s].transpose(0, 1, 3, 2, 4)
            new_n[bi] = No[:, :, s, :, :, 0].transpose(0, 1, 3, 2)
            cps = 256 // LCH
            new_m[bi, :, 0, :] = Mo[:, 0:4, cps * s + cps - 1]
            new_m[bi, :, 1, :] = Mo[:, 32:36, NCH - 1 - cps * s]
    return (y_prompt, y_sample, new_k, new_v, new_C, new_n, new_m)
```

```python
import numpy as np
import ml_dtypes
import concourse.bass as bass
import concourse.mybir as mybir
from concourse.bass_utils import run_bass_kernel_spmd

F32 = mybir.dt.float32
BF16 = mybir.dt.bfloat16
AF = mybir.ActivationFunctionType
ALU = mybir.AluOpType
AX = mybir.AxisListType
_ESZ = {F32: 4, BF16: 2}


def _is_ap(a):
    return hasattr(a, "tensor") and hasattr(a, "ap") and hasattr(a, "offset")


def _box(ap):
    t = ap.tensor
    name = t.name
    esz = _ESZ.get(ap.dtype, 4)
    tsz = _ESZ.get(t.dtype, 4)
    shape = list(t.shape)
    tps = 1
    for s in shape[1:]:
        tps *= int(s)
    tps_b = tps * tsz
    dims = [(int(s), int(c)) for s, c in ap.ap]
    off_b = int(ap.offset) * esz
    p0 = off_b // tps_b
    f0 = off_b % tps_b
    pstep, pcnt = dims[0]
    pstep_b = pstep * esz
    if pstep_b % tps_b == 0 and pstep_b > 0:
        p1 = p0 + (pcnt - 1) * (pstep_b // tps_b) + 1
        rest = dims[1:]
    elif pstep_b == 0:
        p1 = p0 + 1
        rest = dims[1:]
    else:
        p1 = p0 + 1
        rest = dims
    lo = f0
    hi = f0
    for s, c in rest:
        ext = s * (c - 1) * esz
        if ext < 0:
            lo += ext
        else:
            hi += ext
    hi += esz
    if t.__class__.__name__.startswith("PSum"):
        return name, (0, 128, (lo // 2048) * 2048, ((hi + 2047) // 2048) * 2048)
    return name, (p0, p1, lo, hi)


def _overlap(a, b):
    return a[0] < b[1] and b[0] < a[1] and a[2] < b[3] and b[2] < a[3]


def _contains(a, b):
    return a[0] <= b[0] and b[1] <= a[1] and a[2] <= b[2] and b[3] <= a[3]


class _Eng:
    def __init__(self, prog, name):
        self._p = prog
        self._n = name

    def __getattr__(self, op):
        def call(*args, **kw):
            return self._p._emit(self._n, op, args, kw)
        return call


class Prog:
    ENGS = ("pe", "act", "dve", "pool", "sp")

    def __init__(self, nc):
        self.nc = nc
        self.stream = {e: [] for e in self.ENGS}
        self.sems = {}
        self.count = {}
        self.waited = {e: {} for e in self.ENGS}
        self.dma_closed = {}
        self.writes = {}
        self.reads = {}
        self.pe_pending = False
        self._ctx = []
        self._npsum = 0
        for e in self.ENGS:
            self._sem(e)
        self.pe = _Eng(self, "pe")
        self.act = _Eng(self, "act")
        self.dve = _Eng(self, "dve")
        self.pool = _Eng(self, "pool")
        self._consts = {}
        self.ninstr = 0

    def _sem(self, key):
        if key not in self.sems:
            self.sems[key] = self.nc.alloc_semaphore(name="s_" + key.replace(":", "_"))
            self.count[key] = 0
        return self.sems[key]

    def sbuf(self, name, shape, dt):
        self._uid = getattr(self, "_uid", 0) + 1
        name = "%s_%d" % (name, self._uid)
        a0 = int(self.nc.sbuf_base)
        g = self.nc.sbuf_tensor(name, list(shape), dt)
        t = g.__enter__()
        a1 = int(self.nc.sbuf_base)
        self._ctx.append((g, name, a0, a1))
        inh = {}
        for (f0, f1, ticks) in getattr(self, "_freed", ()):
            if f0 < a1 and a0 < f1:
                for k, v in ticks.items():
                    inh[k] = max(inh.get(k, 0), v)
        if inh:
            tsz = 1
            for x in shape[1:]:
                tsz *= int(x)
            full = (0, 128, 0, tsz * _ESZ.get(dt, 4))
            self.reads[name] = [(full, k, v) for k, v in inh.items()]
        return t

    def psum(self, name, shape=(128, 512), dt=F32):
        g = self.nc.psum_tensor(name, list(shape), dt)
        t = g.__enter__()
        self._ctx.append((g, name, -1, -1))
        return t

    def _deps_read(self, ap, deps):
        if not _is_ap(ap) or ap.tensor.__class__.__name__.startswith("DRam"):
            return None
        name, bx = _box(ap)
        for (b, k, v) in self.writes.get(name, ()):
            if _overlap(b, bx):
                deps[k] = max(deps.get(k, 0), v)
        return name, bx

    def _deps_write(self, ap, deps):
        if not _is_ap(ap) or ap.tensor.__class__.__name__.startswith("DRam"):
            return None
        name, bx = _box(ap)
        for (b, k, v) in self.writes.get(name, ()):
            if _overlap(b, bx):
                deps[k] = max(deps.get(k, 0), v)
        for (b, k, v) in self.reads.get(name, ()):
            if _overlap(b, bx):
                deps[k] = max(deps.get(k, 0), v)
        return name, bx

    def _rec_read(self, nb, key, val):
        if nb is None:
            return
        name, bx = nb
        lst = self.reads.setdefault(name, [])
        lst[:] = [r for r in lst if not (r[1] == key and _contains(bx, r[0]))]
        lst.append((bx, key, val))

    def _rec_write(self, nb, key, val):
        if nb is None:
            return
        name, bx = nb
        w = self.writes.setdefault(name, [])
        w[:] = [r for r in w if not _contains(bx, r[0])]
        w.append((bx, key, val))
        r = self.reads.setdefault(name, [])
        r[:] = [x for x in r if not _contains(bx, x[0])]

    def _emit_waits(self, eng, deps):
        for k, v in deps.items():
            if k == "pe" and eng == "pe":
                continue
            if k == "pe" and self.pe_pending and v > self.count["pe"]:
                self._promote_pe()
            if k.startswith("dma:"):
                self.dma_closed[k] = True
                v = self.count[k]
            if self.waited[eng].get(k, 0) >= v:
                continue
            self.waited[eng][k] = v
            sem = self.sems[k]
            self.stream[eng].append(lambda e, sem=sem, v=v: e.wait_ge(sem, v))

    def defer_begin(self):
        self._defer = []

    def defer_end(self):
        ops, self._defer = self._defer, None
        return ops

    def replay(self, ops, n):
        for _ in range(n):
            if not ops:
                return
            kind, a = ops.pop(0)
            if kind == "e":
                self._emit(*a)
            else:
                self.dma(*a[0], **a[1])

    def _emit(self, eng, op, args, kw, inc=True):
        if getattr(self, "_defer", None) is not None:
            self._defer.append(("e", (eng, op, args, kw, inc)))
            return
        outs = []
        ins = []
        first = True
        for a in args:
            if _is_ap(a):
                if first:
                    outs.append(a)
                else:
                    ins.append(a)
                first = False
            elif first and not isinstance(a, (int, float)):
                first = False
        for k, a in kw.items():
            if _is_ap(a):
                if k in ("out", "accum_out"):
                    outs.append(a)
                else:
                    ins.append(a)
        deps = {}
        pin = [a for a in ins if _is_ap(a) and a.tensor.__class__.__name__.startswith("PSum")]
        ins = [a for a in ins if not (_is_ap(a) and a.tensor.__class__.__name__.startswith("PSum"))]
        outs = outs + pin
        rn = [self._deps_read(a, deps) for a in ins]
        wn = [self._deps_write(a, deps) for a in outs]
        self._emit_waits(eng, deps)
        sem = self.sems[eng]
        val = self.count[eng] + 1
        if inc:
            self.count[eng] = val
            if eng == "pe":
                self.pe_pending = False
            self.stream[eng].append(
                lambda e, op=op, args=args, kw=kw, sem=sem: getattr(e, op)(*args, **kw).then_inc(sem, 1))
        else:
            assert eng == "pe"
            self.pe_pending = True
            self._pe_last = (len(self.stream[eng]), op, args, kw)
            self.stream[eng].append(lambda e, op=op, args=args, kw=kw: getattr(e, op)(*args, **kw))
        for nb in rn:
            self._rec_read(nb, eng, val)
        for nb in wn:
            self._rec_write(nb, eng, val)
        self.ninstr += 1

    def _promote_pe(self):
        idx, op, args, kw = self._pe_last
        sem = self.sems["pe"]
        self.stream["pe"][idx] = (
            lambda e, op=op, args=args, kw=kw, sem=sem: getattr(e, op)(*args, **kw).then_inc(sem, 1))
        self.count["pe"] += 1
        self.pe_pending = False

    def mark(self, name):
        if not hasattr(self, "marks"):
            self.marks = []
        self.marks.append((name, getattr(self, "n_mm", 0)))

    def mm(self, out, lhsT, rhs, start=True, stop=True, inc=None):
        self.n_mm = getattr(self, "n_mm", 0) + 1
        if inc is None:
            inc = stop
        self._emit("pe", "matmul", (out, lhsT, rhs), dict(start=start, stop=stop), inc=inc)

    def transpose(self, out, in_, ident, inc=True):
        self.n_mm = getattr(self, "n_mm", 0) + 1
        self._emit("pe", "transpose", (out, in_, ident), {}, inc=inc)

    def dma(self, queue, out, in_, sem=None, slow=False):
        if getattr(self, "_defer", None) is not None:
            self._defer.append(("d", ((queue, out, in_), dict(sem=sem, slow=slow))))
            return
        sb = out if not out.tensor.__class__.__name__.startswith("DRam") else in_
        key = "dma:" + (sem if sem is not None else sb.tensor.name)
        s = self._sem(key)
        deps = {}
        rn = self._deps_read(in_, deps)
        wn = self._deps_write(out, deps)
        if self.dma_closed.get(key, False) and self.count[key] > 0:
            deps[key] = self.count[key]
        self._emit_waits(queue, deps)
        self.dma_closed[key] = False
        self.count[key] += 16
        val = self.count[key]
        kw = dict(allow_slow_non_contiguous=True) if slow else {}
        self.stream[queue].append(
            lambda e, out=out, in_=in_, s=s, kw=kw: e.dma_start(out=out, in_=in_, **kw).then_inc(s, 16))
        self._rec_read(rn, key, val)
        self._rec_write(wn, key, val)
        self.ninstr += 1

    def const_col(self, value):
        if value not in self._consts:
            t = self.sbuf("cst%d" % len(self._consts), [128, 1], F32)
            self.pool.memset(t[:], float(value))
            self._consts[value] = t
        return self._consts[value][:]

    def make_identity(self, t):
        n = int(t.shape[0])
        self.pool.memset(t[:], 1.0)
        self.pool.affine_select(t[:], t[:], [[-1, n]], ALU.is_equal, 0.0, base=0, channel_multiplier=1)

    def finish(self):
        for k in list(self.sems):
            if k.startswith("dma:") and self.count[k] > 0:
                sem, v = self.sems[k], self.count[k]
                self.stream["sp"].append(lambda e, sem=sem, v=v: e.wait_ge(sem, v))
        for k in ("pe", "act", "dve", "pool"):
            if self.count[k] > 0:
                sem, v = self.sems[k], self.count[k]
                self.stream["sp"].append(lambda e, sem=sem, v=v: e.wait_ge(sem, v))
        st = self.stream
        with self.nc.Block() as block:
            @block.sync
            def _(e):
                for f in st["sp"]:
                    f(e)

            @block.tensor
            def _(e):
                for f in st["pe"]:
                    f(e)

            @block.scalar
            def _(e):
                for f in st["act"]:
                    f(e)

            @block.vector
            def _(e):
                for f in st["dve"]:
                    f(e)

            @block.gpsimd
            def _(e):
                for f in st["pool"]:
                    f(e)
        for g in reversed(self._ctx):
            g[0].__exit__(None, None, None)
        self._ctx = []


    def barrier(self):
        if self.pe_pending:
            self._promote_pe()
        for e in self.ENGS:
            deps = {}
            for k in self.sems:
                if self.count[k] > 0:
                    deps[k] = self.count[k]
            self._emit_waits(e, deps)

    def scope_begin(self):
        return len(self._ctx)

    def scope_end(self, mark, barrier=False):
        if barrier:
            self.barrier()
        if self.pe_pending:
            self._promote_pe()
        if not hasattr(self, "_freed"):
            self._freed = []
        while len(self._ctx) > mark:
            g, name, a0, a1 = self._ctx.pop()
            ticks = {}
            for lst in (self.writes.pop(name, []), self.reads.pop(name, [])):
                for (_, k, v) in lst:
                    if k.startswith("dma:"):
                        v = self.count[k]
                    ticks[k] = max(ticks.get(k, 0), v)
            if ticks and a0 >= 0:
                self._freed.append((a0, a1, ticks))
            g.__exit__(None, None, None)


NT = 1024
KT = 16
C_AQ, C_AK, C_AV, C_AG = 0, 1024, 1280, 1536
C_MQ, C_MK, C_MV, C_MO, C_MG, C_MIF, C_FX, C_FG = 2560, 3072, 3584, 4096, 4608, 5120, 5136, 5648
EPS = 1e-6
ATT_SCALE = 128 ** -0.5
MLK_SCALE = 128 ** -0.5
NEG = -30000.0
LCH = 128
NCH = 1024 // LCH
import os as _os
_SKIP = _os.environ.get('KDBG_SKIP', '').split(',')
WIN_BLOCKS = ([(n, g) for g in range(2) for n in ("kv", "ag", "aq")]
              + [(n, 0) for n in ("mq", "mk", "mv", "mo", "mg", "fx", "fg")])
WIN_COL = dict(ag=C_AG, aq=C_AQ, mq=C_MQ, mk=C_MK, mv=C_MV, mo=C_MO, mg=C_MG, fx=C_FX, fg=C_FG)


def mod_queue(nlayers):
    q = [(0, c) for c in range(8, 12)]
    for l in range(1, nlayers):
        q += [(l, c) for c in range(8)]
        q += [(l, c) for c in range(8, 12)]
    return q


def make_plan(nlayers):
    plan = [("mod", 0, c) for c in range(8)]
    q = mod_queue(nlayers)

    def slot():
        if q:
            plan.append(("mod",) + q.pop(0))

    for l in range(nlayers):
        for (n, g) in WIN_BLOCKS:
            plan.append(("in", l, n, g))
            slot()
            if (n, g) == ("aq", 1):
                for cc in range(4):
                    plan.append(("wo", l, 0, cc))
        for cc in range(4):
            plan.append(("wo", l, 1, cc))
    return plan


class WStream:
    NSLOT = 4

    def __init__(self, p, plan, w_in_d, w_out_d, w_mod_d):
        self.p = p
        self.blocks = plan
        self.slots = [p.sbuf("ws%d" % i, [128, 4096], BF16) for i in range(self.NSLOT)]
        self.w_in_d, self.w_out_d, self.w_mod_d = w_in_d, w_out_d, w_mod_d
        self.chunks = []
        self.first = []
        for bi, b in enumerate(plan):
            self.first.append(len(self.chunks))
            n = 1 if (b[0] == "wo" or (b[0] == "in" and b[2] == "kv")) else 2
            for part in range(n):
                self.chunks.append((bi, part))
        self.nload = 0
        self.nuse = 0

    def _slot(self, ci):
        return self.slots[ci % self.NSLOT]

    def _load(self, ci):
        bi, part = self.chunks[ci]
        b = self.blocks[bi]
        s = self._slot(ci)
        p = self.p
        if b[0] == "in" and b[2] == "kv":
            _, l, _, g = b
            v = s[:, :].rearrange("p (k c) -> p k c", c=256)
            for (o, c0) in ((0, C_AK + g * 128), (128, C_AV + g * 128)):
                p.dma("pool", v[:, :, o:o + 128],
                      self.w_in_d[l][:, c0:c0 + 128].rearrange("(k p) c -> p k c", p=128))
            return
        if b[0] == "wo":
            _, l, part_, cc = b
            v = s[:, :].rearrange("p (k c) -> p k c", c=512)
            src = self.w_out_d[l][part_ * 1024:(part_ + 1) * 1024, cc * 512:(cc + 1) * 512]
        elif b[0] == "mod":
            _, l, ci_ = b
            v = s[:, :].rearrange("p (k c) -> p k c", c=256)
            src = self.w_mod_d[l][:, ci_ * 512 + part * 256:ci_ * 512 + (part + 1) * 256]
        else:
            _, l, n, g = b
            c0 = WIN_COL[n] + g * 512 + part * 256
            v = s[:, :].rearrange("p (k c) -> p k c", c=256)
            src = self.w_in_d[l][:, c0:c0 + 256]
        p.dma("pool", v, src.rearrange("(k p) c -> p k c", p=128))

    def advance(self):
        bi = self.nuse - 1
        if bi + 1 >= len(self.first) or self.first[bi + 1] - self.first[bi] != 2:
            return
        lim = min(len(self.chunks), self.first[bi] + self.NSLOT + 1)
        while self.nload < lim:
            self._load(self.nload)
            self.nload += 1

    def get(self, spec):
        bi = self.nuse
        assert self.blocks[bi] == spec, (self.blocks[bi], spec)
        self.nuse += 1
        c0 = self.first[bi]
        while self.nload < min(len(self.chunks), c0 + self.NSLOT):
            self._load(self.nload)
            self.nload += 1
        b = self.blocks[bi]
        if b[0] == "in" and b[2] == "kv":
            v = self._slot(c0)[:, :].rearrange("p (k c) -> p k c", c=256)
            return lambda kt, c, n=128: v[:, kt, c:c + n]
        if b[0] == "wo":
            v = self._slot(c0)[:, :].rearrange("p (k c) -> p k c", c=512)
            return lambda kt, c, n=128: v[:, kt, c:c + n]
        va = self._slot(c0)[:, :].rearrange("p (k c) -> p k c", c=256)
        vb = self._slot(c0 + 1)[:, :].rearrange("p (k c) -> p k c", c=256)
        return lambda kt, c, n=128: (va[:, kt, c:c + n] if c < 256 else vb[:, kt, c - 256:c - 256 + n])


def build_program(nlayers=2, taps=(), stop_after=None):
    nc = bass.Bass("TRN2", target_bir_lowering=False)
    p = Prog(nc)

    def din(name, shape, dt=F32):
        return nc.dram_tensor(name, list(shape), dt, kind="ExternalInput").ap()

    def dout(name, shape, dt=F32):
        return nc.dram_tensor(name, list(shape), dt, kind="ExternalOutput").ap()

    xT_d = din("xT", [128, 16, 1024])
    cond_d = din("condT", [128, 16])
    ckT_d = din("ckT", [2, 2, 128, 512])
    cv_d = din("cv", [2, 2, 512, 128])
    sC_d = din("sC", [2, 2, 128, 4, 128])
    sN_d = din("sN", [2, 2, 128, 4, 1])
    sM_d = din("sM", [2, 64, 1])
    w_in_d = din("w_in", [2, 2048, 6160])
    w_out_d = din("w_out", [2, 2048, 2048])
    w_mod_d = din("w_mod", [2, 2048, 6144])
    normw_d = din("normw", [2, 128, 16])
    bmod_d = din("bmod", [2, 128, 48])
    bI_d = din("bI", [2, 64, 1])
    bF_d = din("bF", [2, 64, 1])
    convw_d = din("convw", [2, 128, 3, 8])
    convb_d = din("convb", [2, 128, 8])
    qnw_d = din("qnw", [2, 128, 1])
    knw_d = din("knw", [2, 128, 1])
    mlnw_d = din("mlnw", [2, 128, 4])
    wfno_d = din("wfno", [2, 4, 128, 128])
    ropeC_d = din("ropeC", [128, 1024], BF16)
    ropeS_d = din("ropeS", [128, 1024], BF16)
    mb_d = din("mb", [128, 48])
    m01_d = din("m01", [128, 48])
    rneg_d = din("rneg", [64, NCH])
    zfl_d = din("zfl", [64, NCH])
    keep_d = din("keep", [64, NCH])
    negbnd_d = din("negbnd", [128, 1])
    scan0_d = din("scan0", [64, 1024], BF16)
    tri_d = din("tri", [LCH, 2, 1, LCH])
    mask8_d = din("mask8", [64, 8, 1])
    tblT_d = din("tblT", [2, 1024, 1024], BF16)
    tblC_d = din("tblC", [128, 256], BF16)
    perm_d = din("perm", [128, 128], BF16)

    yT_o = dout("yT", [128, 16, 1024])
    kT_o = dout("kTo", [2, 2, 128, 1024])
    v_o = dout("vo", [2, 2, 1024, 128])
    C_o = dout("Co", [2, 2, 4, 128, 4, 128])
    N_o = dout("No", [2, 2, 4, 128, 4, 1])
    M_o = dout("Mo", [2, 64, NCH])

    tap_outs = {}

    def tap(name, ap):
        if name not in taps:
            return
        shp = [int(x) for x in ap.shape]
        d = dout("tap_" + name, shp, ap.dtype)
        p.dma("sp", d, ap, sem="tap_" + name)

    xT = p.sbuf("xT_s", [128, 16, 1024], F32)
    hT = p.sbuf("hT", [128, 16, 1024], BF16)
    omlT = p.sbuf("omlT", [128, 4, 1024], BF16)
    ws = WStream(p, make_plan(nlayers), w_in_d, w_out_d, w_mod_d)
    identf = p.sbuf("identf", [128, 128], F32)
    identb = p.sbuf("identb", [128, 128], BF16)
    ones_bf = p.sbuf("ones_bf", [128, 128], BF16)
    twos_bf = p.sbuf("twos_bf", [128, 128], BF16)
    ones_f = p.sbuf("ones_f", [64, 128], F32)
    perm = p.sbuf("perm_s", [128, 128], BF16)
    tblC = p.sbuf("tblC_s", [128, 256], BF16)
    mb = p.sbuf("mb_s", [128, 48], F32)
    m01 = p.sbuf("m01_s", [128, 48, 1], BF16)
    m01f = p.sbuf("m01f_s", [128, 48], F32)
    condT = p.sbuf("condT_s", [128, 16], F32)
    sc = p.sbuf("sc", [128, 16], BF16)
    modv = [p.sbuf("modv%d" % l, [128, 48], F32) for l in range(2)]
    normw = p.sbuf("normw_s", [128, 2, 16], F32)
    bmod = p.sbuf("bmod_s", [128, 2, 48], F32)
    acol = p.sbuf("acol", [128, 16], F32)
    convw = p.sbuf("convw_s", [128, 2, 3, 8], F32)
    convb = p.sbuf("convb_s", [128, 2, 8], F32)
    hw = p.sbuf("hw", [128, 2, 3, 8], F32)
    hb = p.sbuf("hb", [128, 2, 8], F32)
    nb0 = p.sbuf("nb0", [128, 2, 8], F32)
    nb2 = p.sbuf("nb2", [128, 2, 8], F32)
    negbnd = p.sbuf("negbnd_s", [128, 1], F32)
    qnw = p.sbuf("qnw_s", [128, 2, 1], F32)
    knw = p.sbuf("knw_s", [128, 2, 1], F32)
    mlnw = p.sbuf("mlnw_s", [128, 2, 4], F32)
    mlnw4 = p.sbuf("mlnw4", [128, 2, 4], F32)
    bI = p.sbuf("bI_s", [64, 2, 1], F32)
    bF = p.sbuf("bF_s", [64, 2, 1], F32)
    sM = p.sbuf("sM_s", [64, 2, 1], F32)
    rneg = p.sbuf("rneg_s", [64, NCH], F32)
    zfl = p.sbuf("zfl_s", [64, NCH], F32)
    keep = p.sbuf("keep_s", [64, NCH], F32)
    tri = p.sbuf("tri_s", [LCH, 2, 1, LCH], F32)
    mask8 = p.sbuf("mask8_s", [64, 8, 1], F32)
    wfno = p.sbuf("wfno_s", [128, 4, 128], BF16)

    PD = [p.psum("pd%d" % i, (128, 1024)) for i in range(4)]
    PS = [PD[i // 2][:, (i % 2) * 512:(i % 2 + 1) * 512] for i in range(8)]
    PSb = [PS[i][:, :].bitcast(BF16) for i in range(8)]

    H = [slice(0, 512), slice(512, 1024)]

    for q4 in range(4):
        p.dma("sp", xT[:, 4 * q4:4 * q4 + 4, :], xT_d[:, 4 * q4:4 * q4 + 4, :])
    cl = lambda dst, src: p.dma("sp", dst, src, sem="consts")
    cl(condT[:], cond_d)
    for l in range(2):
        cl(normw[:, l, :], normw_d[l])
        cl(bmod[:, l, :], bmod_d[l])
        cl(convw[:, l], convw_d[l])
        cl(convb[:, l, :], convb_d[l])
        cl(qnw[:, l, :], qnw_d[l])
        cl(knw[:, l, :], knw_d[l])
        cl(mlnw[:, l, :], mlnw_d[l])
        cl(bI[:, l, :], bI_d[l])
        cl(bF[:, l, :], bF_d[l])
        cl(sM[:, l, :], sM_d[l])
    cl(perm[:], perm_d)
    cl(tblC[:], tblC_d)
    cl(mb[:], mb_d)
    cl(m01f[:], m01_d)
    cl(negbnd[:], negbnd_d)
    cl(rneg[:], rneg_d)
    cl(zfl[:], zfl_d)
    cl(keep[:], keep_d)
    cl(tri[:], tri_d)
    cl(mask8[:], mask8_d)

    p.make_identity(identf)
    p.dve.tensor_copy(m01[:].rearrange("p a o -> p (a o)"), m01f[:])
    p.dve.tensor_copy(identb[:], identf[:])
    p.pool.memset(ones_bf[:], 1.0)
    p.pool.memset(twos_bf[:], 2.0)
    p.pool.memset(ones_f[:], 1.0)
    eps_col = p.const_col(EPS)
    one_col = p.const_col(1.0)

    p.dve.tensor_scalar(hw[:], convw[:], 0.5, None, ALU.mult)
    p.dve.tensor_scalar(hb[:], convb[:], 0.5, None, ALU.mult)
    p.dve.tensor_scalar(nb0[:], hw[:, :, 0, :], negbnd[:, 0:1], None, ALU.mult)
    p.dve.tensor_scalar(nb2[:], hw[:, :, 2, :], negbnd[:, 0:1], None, ALU.mult)
    p.dve.tensor_scalar(mlnw4[:], mlnw[:], 0.25, None, ALU.mult)
    m0 = p.scope_begin()
    tcond = p.sbuf("tcond", [128, 16], F32)
    p.act.activation(tcond[:], condT[:], AF.Tanh, scale=0.5)
    p.dve.scalar_tensor_tensor(tcond[:], tcond[:], 1.0, condT[:], ALU.add, ALU.mult)
    p.dve.tensor_scalar(sc[:], tcond[:], 0.5, None, ALU.mult)
    p.scope_end(m0)

    mod_next = {}

    def emit_mod_chunk(l, ci):
        w = ws.get(("mod", l, ci))
        for j in range(4):
            mt = 4 * ci + j
            for kt in range(KT):
                p.mm(PS[6][:, 64 + mt:65 + mt], w(kt, j * 128), sc[:, kt:kt + 1],
                     start=(kt == 0), stop=(kt == KT - 1))
            if j == 1:
                ws.advance()
        p.dve.tensor_tensor(modv[l][:, 4 * ci:4 * ci + 4], PS[6][:, 64 + 4 * ci:68 + 4 * ci],
                            bmod[:, l, 4 * ci:4 * ci + 4], ALU.add)

    modq = mod_queue(nlayers)

    def maybe_mod(l=None):
        if modq:
            emit_mod_chunk(*modq.pop(0))

    def gate_col(l, ct):
        return modv[l][:, 32 + ct:33 + ct]

    def phase_norm(l, between=None):
        p.mark("norm%d" % l)
        m = p.scope_begin()
        SQ = [p.sbuf("nsq%d" % i, [128, 1024], BF16) for i in range(2)]
        rstd = p.sbuf("nrstd", [128, 1024], F32)
        TMP = [p.sbuf("ntmp%d" % i, [128, 1024], F32) for i in range(2)]
        for kt in range(KT):
            sq = SQ[kt % 2]
            p.act.activation(sq[:], xT[:, kt, :], AF.Square)
            for hf in range(2):
                p.mm(PS[hf][:, :], ones_bf[:], sq[:, H[hf]], start=(kt == 0), stop=(kt == KT - 1))
        for hf in range(2):
            p.act.activation(rstd[:, H[hf]], PS[hf][:, :], AF.Ln, bias=eps_col, scale=1.0 / 2048)
        p.act.activation(rstd[:], rstd[:], AF.Exp, scale=-0.5)
        if between is not None:
            between()
        p.dve.scalar_tensor_tensor(acol[:], modv[l][:, 16:32], 1.0, normw[:, l, :], ALU.add, ALU.mult)
        for kt in range(KT):
            tmp = TMP[kt % 2]
            p.dve.tensor_tensor(tmp[:], xT[:, kt, :], rstd[:], ALU.mult)
            p.act.activation(hT[:, kt, :], tmp[:], AF.Identity, bias=modv[l][:, kt:kt + 1],
                             scale=acol[:, kt:kt + 1])
        tap("hT%d" % l, hT[:])
        p.scope_end(m)

    def proj_tile(w, c0, banks):
        for hf in range(2):
            ps = PS[banks[hf]]
            for kt in range(KT):
                p.mm(ps[:, :], w(kt, c0), hT[:, kt, H[hf]], start=(kt == 0), stop=(kt == KT - 1))
        return PS[banks[0]], PS[banks[1]]

    def wout_pass(l, part, rhs_fn):
        for cc in range(4):
            w = ws.get(("wo", l, part, cc))
            for ci in range(4):
                ct = cc * 4 + ci
                for hf in range(2):
                    ps = PS[(ci * 2 + hf) % 4]
                    for k in range(8):
                        p.mm(ps[:, :], w(k, ci * 128), rhs_fn(k, hf),
                             start=(k == 0), stop=(k == 7))
                    p.dve.scalar_tensor_tensor(xT[:, ct, H[hf]], ps[:, :], gate_col(l, ct),
                                               xT[:, ct, H[hf]], ALU.mult, ALU.add)
            if part == 1 and l == nlayers - 1 and stop_after is None:
                p.dma("sp", yT_o[:, cc * 4:(cc + 1) * 4, :], xT[:, cc * 4:(cc + 1) * 4, :], sem="yout")

    def attention_phase(l):
        m = p.scope_begin()
        GO = p.sbuf("GO", [128, 8, 1024], BF16)
        KT_ = p.sbuf("KT_", [128, 1536], BF16)
        Vt = p.sbuf("Vt", [128, 12, 128], BF16)
        QT = p.sbuf("QT", [128, 4, 1024], BF16)
        ropeC = p.sbuf("ropeC_s", [128, 1024], BF16)
        ropeS = p.sbuf("ropeS_s", [128, 1024], BF16)
        p.dma("sp", ropeC[:], ropeC_d)
        p.dma("sp", ropeS[:], ropeS_d)
        PSn = (PS[6], PS[7])

        for g in range(2):
            ma = p.scope_begin()
            sqb = p.sbuf("sqb", [128, 1024], BF16)
            rs = p.sbuf("rs", [128, 1024], F32)
            qn = [p.sbuf("qn%d" % a, [128, 1024], BF16) for a in range(2)]
            t1 = p.sbuf("t1", [128, 1024], F32)
            t2 = p.sbuf("t2", [128, 1024], F32)
            kf = p.sbuf("kf", [128, 1024], F32)
            vst = p.sbuf("vst", [128, 512], F32)
            PDn = PD[3]

            def qk_partA(psq):
                p.act.activation(sqb[:], psq[:, :], AF.Square)
                for hf in range(2):
                    p.mm(PSn[hf][:, :], ones_bf[:], sqb[:, H[hf]])

            def qk_partB(psq, wcol, par):
                p.act.activation(rs[:], PDn[:, :], AF.Ln, bias=eps_col, scale=1.0 / 128)
                p.act.activation(rs[:], rs[:], AF.Exp, scale=-0.5)
                p.dve.scalar_tensor_tensor(qn[par][:], psq[:, :], wcol, rs[:], ALU.mult, ALU.mult)

            def qk_part2(par, dst, f32out=None):
                for hf in range(2):
                    p.mm(PSn[hf][:, :], perm[:], qn[par][:, H[hf]])
                p.dve.tensor_tensor(t1[:], qn[par][:], ropeC[:, :], ALU.mult)
                p.dve.tensor_tensor(t2[:], PDn[:, :], ropeS[:, :], ALU.mult)
                if f32out is None:
                    p.dve.tensor_tensor(dst, t1[:], t2[:], ALU.add)
                else:
                    p.dve.tensor_tensor(f32out, t1[:], t2[:], ALU.add)
                    p.act.copy(dst, f32out)

            w = ws.get(("in", l, "kv", g))
            p.dma("pool", KT_[:, 0:512], ckT_d[l, g])
            p.dma("pool", Vt[:, 0:4, :], cv_d[l, g].rearrange("(k p) d -> p k d", p=128))
            proj_tile(w, 0, (0, 1))
            qk_partA(PD[0])
            for tt in range(8):
                ps = PS[2 + tt // 4]
                c0 = (tt % 4) * 128
                for kt in range(KT):
                    p.mm(ps[:, c0:c0 + 128], hT[:, kt, tt * 128:(tt + 1) * 128], w(kt, 128),
                         start=(kt == 0), stop=(kt == KT - 1))
            qk_partB(PD[0], knw[:, l, :], 0)
            for b in range(2):
                p.act.copy(vst[:], PS[2 + b][:, :])
                p.dma("sp", v_o[l, g, b * 512:(b + 1) * 512, :].rearrange("(k p) d -> p k d", p=128),
                      vst[:].rearrange("p (k d) -> p k d", d=128))
                p.dve.tensor_copy(Vt[:, 4 + 4 * b:8 + 4 * b, :], vst[:].rearrange("p (k d) -> p k d", d=128))
            maybe_mod(l)
            w = ws.get(("in", l, "ag", g))
            for h in range(4):
                proj_tile(w, h * 128, ((0, 1) if h % 2 == 0 else (2, 3)))
                psg = PD[h % 2]
                if h == 1:
                    ws.advance()
                if h == 0:
                    qk_part2(0, KT_[:, 512:1536], f32out=kf[:])
                    p.dma("sp", kT_o[l, g], kf[:])
                p.act.activation(t1[:], psg[:, :], AF.Tanh, scale=0.5)
                p.dve.scalar_tensor_tensor(GO[:, 4 * g + h, :], t1[:], 1.0, psg[:, :], ALU.add, ALU.mult)
            maybe_mod(l)
            w = ws.get(("in", l, "aq", g))
            bank = lambda h: ((0, 1) if h % 2 == 0 else (2, 3))
            proj_tile(w, 0, bank(0))
            for h in range(4):
                qk_partA(PD[h % 2])
                if h + 1 < 4:
                    proj_tile(w, (h + 1) * 128, bank(h + 1))
                    if h + 1 == 1:
                        ws.advance()
                qk_partB(PD[h % 2], qnw[:, l, :], h % 2)
                qk_part2(h % 2, QT[:, h, :])
            maybe_mod(l)
            p.scope_end(ma)
            if g == 0:
                tap("KT%d" % l, KT_[:])
                tap("QT%d" % l, QT[:])
                tap("Vt%d" % l, Vt[:])
            p.mark("att%d.g%d.loop" % (l, g))
            mb_ = p.scope_begin()
            PT = [p.sbuf("PT%d" % i, [128, 1024], BF16) for i in range(4)]
            rL = [[p.sbuf("rL%d%d" % (a, i), [128, 512], F32) for i in range(2)] for a in range(2)]
            oS = [[p.sbuf("oS%d%d" % (a, i), [128, 512], F32) for i in range(2)] for a in range(2)]
            steps = [(h, kt) for h in range(4) for kt in range(12)]
            Sb = (PD[1], PD[3])
            psO = (PS[0], PS[1])
            psL = (PS[4], PS[5])

            def emit_S(i):
                h, kt = steps[i]
                for qc in range(2):
                    p.mm(Sb[i % 2][:, H[qc]], KT_[:, kt * 128:(kt + 1) * 128], QT[:, h, H[qc]], inc=(qc == 1))

            def emit_E(i):
                h, kt = steps[i]
                pt = PT[i % 4]
                if kt < 4:
                    p.act.activation(pt[:], Sb[i % 2][:, :], AF.Exp, bias=mb[:, kt * 4:kt * 4 + 1], scale=ATT_SCALE)
                else:
                    p.act.activation(pt[:], Sb[i % 2][:, :], AF.Exp, scale=ATT_SCALE)
                    for qb in range(4):
                        blk = pt[:, qb * 256:(qb + 1) * 256]
                        p.dve.tensor_scalar(blk, blk, m01f[:, kt * 4 + qb:kt * 4 + qb + 1], None, ALU.mult)

            def emit_PV(i):
                h, kt = steps[i]
                pt = PT[i % 4]
                for qc in range(2):
                    p.mm(psO[qc][:, :], Vt[:, kt, :], pt[:, H[qc]], start=(kt == 0), stop=(kt == 11))
                for qc in range(2):
                    p.mm(psL[qc][:, :], twos_bf[:], pt[:, H[qc]], start=(kt == 0), stop=(kt == 11))
                if kt == 11:
                    hb = h % 2
                    for qc in range(2):
                        p.act.activation(rL[hb][qc][:], psL[qc][:, :], AF.Ln)
                        p.act.activation(rL[hb][qc][:], rL[hb][qc][:], AF.Exp, scale=-1.0)
                        p.dve.tensor_copy(oS[hb][qc][:], psO[qc][:, :])
                    for qc in range(2):
                        p.pool.tensor_tensor(rL[hb][qc][:], rL[hb][qc][:], GO[:, 4 * g + h, H[qc]], ALU.mult)

                    def fin(h=h, hb=hb):
                        for qc in range(2):
                            p.dve.tensor_tensor(GO[:, 4 * g + h, H[qc]], oS[hb][qc][:], rL[hb][qc][:], ALU.mult)
                    pending.append([3, fin])

            pending = []
            emit_S(0)
            for i in range(len(steps)):
                if i + 1 < len(steps):
                    emit_S(i + 1)
                emit_E(i)
                for pe_ in list(pending):
                    pe_[0] -= 1
                    if pe_[0] <= 0:
                        pe_[1]()
                        pending.remove(pe_)
                if i >= 1:
                    emit_PV(i - 1)
            emit_PV(len(steps) - 1)
            for pe_ in pending:
                pe_[1]()
            p.scope_end(mb_)
        p.mark("att%d.wout" % l)
        tap("oatt%d" % l, GO[:])
        wout_pass(l, 0, lambda k, hf: GO[:, k, H[hf]])
        tap("xatt%d" % l, xT[:])
        p.scope_end(m)

    def mlstm_phase(l):
        p.mark("ml%d.qkconv" % l)
        mo_ = p.scope_begin()
        SCR = p.sbuf("SCR", [128, 4096], F32)
        T = [SCR[:, i * 1024:(i + 1) * 1024] for i in range(4)]
        Hs = SCR[0:LCH, 0:NCH * 256].bitcast(BF16).rearrange("p (c h d) -> p c h d", h=4, d=128)
        mi = p.scope_begin()
        qT = p.sbuf("mqT", [128, 4, 1024], BF16)
        ksT = p.sbuf("mksT", [128, 4, 1024], BF16)
        Vtok = p.sbuf("Vtok", [LCH, NCH, 4, 128], BF16)
        cwtok = p.sbuf("cwtok", [LCH, NCH, 8, 1], F32)
        ebtok = p.sbuf("ebtok", [LCH, NCH, 8, 1], F32)
        cwb = p.sbuf("cwb", [LCH, NCH, 8, 1], BF16)
        decbc = p.sbuf("decbc", [128, 8, NCH], F32)

        def conv_tile(ct, psq, dst):
            X, acc, tt_ = T[0], T[1], T[2]
            p.act.copy(X, psq[:, :])
            p.dve.tensor_scalar(acc, X, hw[:, l, 1, ct:ct + 1], hb[:, l, ct:ct + 1], ALU.mult, ALU.add)
            p.dve.scalar_tensor_tensor(acc[:, 1:1024], X[:, 0:1023], hw[:, l, 0, ct:ct + 1], acc[:, 1:1024],
                                       ALU.mult, ALU.add)
            p.dve.scalar_tensor_tensor(acc[:, 0:1023], X[:, 1:1024], hw[:, l, 2, ct:ct + 1], acc[:, 0:1023],
                                       ALU.mult, ALU.add)
            p.dve.scalar_tensor_tensor(acc[:, 256:1024:256], X[:, 255:1023:256], nb0[:, l, ct:ct + 1],
                                       acc[:, 256:1024:256], ALU.mult, ALU.add)
            p.dve.scalar_tensor_tensor(acc[:, 255:1023:256], X[:, 256:1024:256], nb2[:, l, ct:ct + 1],
                                       acc[:, 255:1023:256], ALU.mult, ALU.add)
            p.act.activation(tt_, acc, AF.Tanh)
            p.dve.scalar_tensor_tensor(dst, tt_, 1.0, acc, ALU.add, ALU.mult)

        for (nm, dstT, cto) in (("mq", qT, 0), ("mk", ksT, 4)):
            w = ws.get(("in", l, nm, 0))
            for h in range(4):
                pss = proj_tile(w, h * 128, ((0, 1) if h % 2 == 0 else (2, 3)))
                if h == 1:
                    ws.advance()
                conv_tile(cto + h, PD[h % 2], dstT[:, h, :])
            maybe_mod(l)
        tap("mqT%d" % l, qT[:])
        tap("mksT%d" % l, ksT[:])

        p.mark("ml%d.gates" % l)
        mg_ = p.scope_begin()
        tot = p.sbuf("g_tot", [64, NCH], F32)
        amax = p.sbuf("g_amax", [64, NCH], F32)
        d0 = p.sbuf("g_d0", [64, NCH], F32)
        d1 = p.sbuf("g_d1", [64, NCH], F32)
        mnext = p.sbuf("g_mnext", [64, NCH], F32)
        min_ = p.sbuf("g_min", [64, NCH], F32)
        Ag = p.sbuf("g_A", [64, NCH, 1], F32)
        dec = p.sbuf("g_dec", [64, NCH], F32)
        decn = p.sbuf("g_decn", [64, 1, NCH], F32)
        Eg = p.sbuf("g_E", [64, 8, NCH], F32)
        scan0 = p.sbuf("scan0_s", [64, 1024], BF16)
        WgI = p.sbuf("WgI", [128, 16, 64], BF16)
        WgF = p.sbuf("WgF", [128, 16, 64], BF16)
        Wg16 = p.sbuf("Wg16", [128, 16, 16], F32)
        p.dma("sp", scan0[:], scan0_d)
        p.pool.memset(WgI[:], 0.0)
        p.pool.memset(WgF[:], 0.0)
        p.dma("sp", Wg16[:], w_in_d[l][:, C_MIF:C_MIF + 16].rearrange("(k p) c -> p k c", p=128))
        p.pool.tensor_copy(WgI[:, :, 0:4], Wg16[:, :, 0:4])
        p.pool.tensor_copy(WgI[:, :, 32:36], Wg16[:, :, 8:12])
        p.pool.tensor_copy(WgF[:, :, 0:4], Wg16[:, :, 4:8])
        p.pool.tensor_copy(WgF[:, :, 32:36], Wg16[:, :, 12:16])
        for hf in range(2):
            for kt in range(KT):
                p.mm(PS[hf][0:64, :], WgI[:, kt, :], hT[:, kt, H[hf]], start=(kt == 0), stop=(kt == KT - 1))
            for kt in range(KT):
                p.mm(PS[2 + hf][0:64, :], WgF[:, kt, :], hT[:, kt, H[hf]], start=(kt == 0), stop=(kt == KT - 1))
        T0, T1, T2, T3 = [t[0:64, :] for t in T]
        for hf in range(2):
            p.act.activation(T0[:, H[hf]], PS[hf][0:64, :], AF.Identity, bias=bI[:, l, :])
            p.act.activation(T1[:, H[hf]], PS[2 + hf][0:64, :], AF.Identity, bias=bF[:, l, :])
        p.defer_begin()
        p.act.activation(T1, T1, AF.Exp, scale=-1.0)
        p.act.activation(T1, T1, AF.Ln, bias=one_col[0:64, :], scale=1.0)
        for Tx in (T[0], T[1]):
            p.dve.tensor_copy(T[2][32:36, :], Tx[32:36, ::-1])
            p.dve.tensor_copy(Tx[32:36, :], T[2][32:36, :])
        p.dve.tensor_tensor_scan(T3, scan0[:, :], T1, 0.0, ALU.mult, ALU.add)
        p.dve.tensor_tensor(T0, T0, T3, ALU.add)
        c3 = lambda ap: ap.rearrange("p (c l) -> p c l", l=LCH)
        p.dve.tensor_copy(tot[:], c3(T3)[:, :, LCH - 1])
        p.dve.tensor_reduce(amax[:], c3(T0), AX.X, ALU.max)
        p.dve.tensor_tensor(d0[:], rneg[:], tot[:], ALU.subtract)
        p.dve.tensor_tensor(d1[:], amax[:], zfl[:], ALU.max)
        p.dve.tensor_tensor(d1[:], d1[:], tot[:], ALU.subtract)
        p.dve.tensor_tensor_scan(mnext[:], d0[:], d1[:], sM[:, l, :], ALU.add, ALU.max)
        p.dma("sp", M_o[l], mnext[:])
        p.dve.tensor_copy(min_[:, 0:1], sM[:, l, :])
        p.dve.tensor_copy(min_[:, 1:NCH], mnext[:, 0:NCH - 1])
        p.dve.tensor_tensor(min_[:], min_[:], keep[:], ALU.mult)
        A2 = Ag[:].rearrange("p c o -> p (c o)")
        p.dve.tensor_tensor(A2, min_[:], amax[:], ALU.max)
        p.dve.tensor_tensor(dec[:], min_[:], A2, ALU.subtract)
        p.act.activation(dec[:], dec[:], AF.Exp)
        p.dve.tensor_tensor(dec[:], dec[:], keep[:], ALU.mult)
        Abc = Ag[:].broadcast_to([64, NCH, LCH])
        p.dve.tensor_tensor(c3(T0), c3(T0), Abc, ALU.subtract)
        p.act.activation(T0, T0, AF.Exp)
        p.dve.tensor_tensor(c3(T3), c3(T3), Abc, ALU.subtract)
        p.act.activation(T3, T3, AF.Exp)
        for Tx in (T[0], T[3]):
            p.dve.tensor_copy(T[2][32:36, :], Tx[32:36, ::-1])
            p.dve.tensor_copy(Tx[32:36, :], T[2][32:36, :])
        dn2 = decn[:].rearrange("p o c -> p (o c)")
        p.dve.tensor_copy(dn2, dec[:])
        p.dve.tensor_copy(dn2[32:36, :], dec[32:36, ::-1])
        gate_chain = p.defer_end()
        p.mark("ml%d.vproj" % l)
        w = ws.get(("in", l, "mv", 0))
        for j in range(NCH):
            ps = PS[2 + j % 2]
            for cc in range(2):
                for kt in range(KT):
                    p.mm(ps[0:LCH, cc * 256:(cc + 1) * 256], hT[:, kt, j * LCH:(j + 1) * LCH], w(kt, cc * 256, 256),
                         start=(kt == 0), stop=(kt == KT - 1))
            p.act.copy(Vtok[:, j, :, :], ps[0:LCH, :].rearrange("t (h d) -> t h d", d=128))
            p.replay(gate_chain, 7)
        p.replay(gate_chain, 10 ** 6)
        maybe_mod(l)
        for (Tx, dst) in ((T0, cwtok), (T3, ebtok)):
            for b in range(NCH // 8):
                ps = PS[4 + b]
                for c8 in range(8):
                    c = b * 8 + c8
                    p.transpose(ps[0:LCH, c8 * 64:(c8 + 1) * 64], Tx[:, c * LCH:(c + 1) * LCH], identf[0:64, 0:64],
                                inc=(c8 == 7))
                psv = ps[0:LCH, :].rearrange("t (c q) -> t c q", q=64)
                d3 = dst[:].rearrange("t c q o -> t c (q o)")
                p.dve.tensor_copy(d3[:, b * 8:(b + 1) * 8, 0:4], psv[:, :, 0:4])
                p.dve.tensor_copy(d3[:, b * 8:(b + 1) * 8, 4:8], psv[:, :, 32:36])
        p.dve.tensor_copy(cwb[:], cwtok[:])
        p.dve.tensor_tensor(Eg[:], mask8[:].broadcast_to([64, 8, NCH]), decn[:].broadcast_to([64, 8, NCH]), ALU.mult)
        p.mm(PS[6][:, 0:8 * NCH], ones_f[:, :], Eg[:].rearrange("p a b -> p (a b)"))
        p.act.copy(decbc[:].rearrange("p a b -> p (a b)"), PS[6][:, 0:8 * NCH])
        p.scope_end(mg_)

        p.mark("ml%d.chains" % l)
        CN = [[p.sbuf("CN%d%d" % (d, i), [128, 4, 129], F32) for i in range(2)] for d in range(2)]
        CNb = [p.sbuf("CNb%d" % d, [128, 4, 129], BF16) for d in range(2)]
        VP = [[p.sbuf("VP%d%d" % (d, i), [LCH, 4, 128], BF16) for i in range(2)] for d in range(2)]
        SM = [[p.sbuf("SM%d%d" % (d, i), [LCH, 4, LCH], BF16) for i in range(2)] for d in range(2)]
        KTOK = [[p.sbuf("KTOK%d" % d, [LCH, 4, 128], BF16)] * 2 for d in range(2)]
        dn = [p.sbuf("g_dn%d" % d, [LCH, 4, 1], F32) for d in range(2)]
        hbuf = [p.sbuf("g_hb%d" % d, [LCH, 4, 128], BF16) for d in range(2)]
        for dr in range(2):
            p.dma("sp", CN[dr][0][:, :, 0:128], sC_d[l, dr])
            p.dma("sp", CN[dr][0][:, :, 128:129], sN_d[l, dr], slow=True)
        psS = (PS[0], PS[1])
        psN = (PS[2], PS[3])
        psU = (PS[4], PS[5])
        jof = lambda dr, i: (i if dr == 0 else NCH - 1 - i)

        def stage_A1(dr, i):
            j = jof(dr, i)
            cs = slice(j * LCH, (j + 1) * LCH)
            vp, sm, kt_ = VP[dr][i % 2], SM[dr][i % 2], KTOK[dr][i % 2]
            p.pool.tensor_tensor(vp[:], Vtok[:, j, :, :],
                                 cwtok[:, j, dr * 4:(dr + 1) * 4, :].broadcast_to([LCH, 4, 128]), ALU.mult)
            for h in range(4):
                p.mm(psS[dr][0:LCH, h * LCH:(h + 1) * LCH], ksT[:, h, cs], qT[:, h, cs], inc=(h == 3))
            p.dve.scalar_tensor_tensor(sm[:], psS[dr][0:LCH, 0:4 * LCH].rearrange("s (h l) -> s h l", l=LCH), MLK_SCALE,
                                       tri[:, dr, :, :].broadcast_to([LCH, 4, LCH]), ALU.mult, ALU.mult)
            psT = PSb[7]
            o = dr * 512
            for h in range(4):
                p.transpose(psT[0:LCH, o + h * 128:o + (h + 1) * 128], ksT[:, h, cs], identb[:, :], inc=(h == 3))
            p.act.mul(kt_[:].rearrange("s h d -> s (h d)"), psT[0:LCH, o:o + 512], MLK_SCALE)

        def stage_A2(dr, i):
            j = jof(dr, i)
            vp, kt_ = VP[dr][i % 2], KTOK[dr][i % 2]
            for h in range(4):
                p.mm(psU[dr][:, h * 128:(h + 1) * 128], kt_[:, h, :], vp[:, h, :], inc=(h == 3))
            co = 16 + 8 * dr
            for h in range(4):
                p.mm(PS[6][:, co + h:co + h + 1], kt_[:, h, :], cwb[:, j, dr * 4 + h, :], inc=(h == 3))

        def stage_B(dr, i):
            j = jof(dr, i)
            cs = slice(j * LCH, (j + 1) * LCH)
            vp, sm = VP[dr][i % 2], SM[dr][i % 2]
            cur, cn, cnb = CN[dr][i % 2], CN[dr][(i + 1) % 2], CNb[dr]
            co = 16 + 8 * dr
            cd = 8 * dr
            for h in range(4):
                p.act.activation(cnb[:, h, :], cur[:, h, :], AF.Identity, scale=decbc[:, dr * 4 + h, j:j + 1])
            for h in range(4):
                p.mm(psN[dr][0:LCH, h * 128:(h + 1) * 128], qT[:, h, cs], cnb[:, h, 0:128], start=True, stop=False,
                     inc=False)
                p.mm(psN[dr][0:LCH, h * 128:(h + 1) * 128], sm[:, h, :], vp[:, h, :], start=False, stop=True,
                     inc=(h == 3))
            for h in range(4):
                p.mm(PS[6][0:LCH, cd + h:cd + h + 1], qT[:, h, cs], cnb[:, h, 128:129], start=True, stop=False,
                     inc=False)
                p.mm(PS[6][0:LCH, cd + h:cd + h + 1], sm[:, h, :], cwb[:, j, dr * 4 + h, :], start=False,
                     stop=True, inc=(h == 3))
            for h in range(4):
                p.dve.scalar_tensor_tensor(cn[:, h, 0:128], cur[:, h, 0:128], decbc[:, dr * 4 + h, j:j + 1],
                                           psU[dr][:, h * 128:(h + 1) * 128], ALU.mult, ALU.add)
            p.dve.tensor_tensor(cn[:, :, 128:129], cur[:, :, 128:129], decbc[:, dr * 4:(dr + 1) * 4, j:j + 1],
                                ALU.mult)
            p.dve.tensor_tensor(cn[:, :, 128:129], cn[:, :, 128:129],
                                PS[6][:, co:co + 4].rearrange("p (h o) -> p h o", o=1), ALU.add)
            den = PS[6][0:LCH, cd:cd + 4].rearrange("p (h o) -> p h o", o=1)
            d_ = dn[dr]
            p.dve.tensor_tensor(d_[:], den, ebtok[:, j, dr * 4:(dr + 1) * 4, :], ALU.max)
            p.dve.scalar_tensor_tensor(d_[:], den, -1.0, d_[:], ALU.mult, ALU.max)
            p.dve.reciprocal(d_[:], d_[:])
            num3 = psN[dr][0:LCH, :].rearrange("p (h d) -> p h d", d=128)
            if i < NCH // 2:
                p.dve.tensor_tensor(Hs[:, j], num3, d_[:].broadcast_to([LCH, 4, 128]), ALU.mult)
            else:
                p.dve.tensor_tensor(hbuf[dr][:], num3, d_[:].broadcast_to([LCH, 4, 128]), ALU.mult)
                p.pool.tensor_tensor(Hs[:, j], Hs[:, j], hbuf[dr][:], ALU.add)
            cps = 256 // LCH
            if (dr == 0 and j % cps == cps - 1) or (dr == 1 and j % cps == 0):
                s_ = j // cps
                p.dma("sp", C_o[l, dr, s_], cn[:, :, 0:128], sem="CNo%d" % dr)
                p.dma("sp", N_o[l, dr, s_], cn[:, :, 128:129], sem="CNo%d" % dr, slow=True)

        for dr in range(2):
            stage_A1(dr, 0)
            stage_A2(dr, 0)
        for i in range(NCH):
            for dr in range(2):
                if i + 1 < NCH:
                    stage_A1(dr, i + 1)
                stage_B(dr, i)
                if i + 1 < NCH:
                    stage_A2(dr, i + 1)
        tap("Hs%d" % l, Hs)
        p.scope_end(mi)

        p.mark("ml%d.out" % l)
        mi = p.scope_begin()
        Gm = p.sbuf("Gm", [128, 4, 1024], BF16)
        tg = p.sbuf("mtg", [128, 512], F32)
        tu = p.sbuf("mtu", [128, 512], F32)
        ssq = p.sbuf("mssq", [LCH, NCH * 4, 1], F32)
        junk = p.sbuf("mjunk", [LCH, 128], BF16)
        ssq_jobs = [(j, h) for j in range(NCH) for h in range(4)]

        def some_ssq(n):
            for _ in range(n):
                if ssq_jobs:
                    j, h = ssq_jobs.pop(0)
                    p.act.activation(junk[:], Hs[:, j, h, :], AF.Square, accum_out=ssq[:, j * 4 + h, :])

        some_ssq(NCH * 4 // 8)
        w = ws.get(("in", l, "mo", 0))
        for h in range(4):
            pss = proj_tile(w, h * 128, (0, 1))
            if h == 1:
                ws.advance()
            for hf in range(2):
                p.act.activation(Gm[:, h, H[hf]], pss[hf][:, :], AF.Tanh, scale=0.5)
            some_ssq(NCH * 4 // 8)
        maybe_mod(l)
        w = ws.get(("in", l, "mg", 0))
        for h in range(4):
            pss = proj_tile(w, h * 128, (2, 3))
            if h == 1:
                ws.advance()
            for hf in range(2):
                p.act.activation(tg[:], pss[hf][:, :], AF.Tanh, scale=0.5)
                p.dve.scalar_tensor_tensor(tu[:], tg[:], 1.0, pss[hf][:, :], ALU.add, ALU.mult)
                p.dve.scalar_tensor_tensor(Gm[:, h, H[hf]], Gm[:, h, H[hf]], 1.0, tu[:], ALU.add, ALU.mult)
            some_ssq(NCH * 4 // 8)
        some_ssq(NCH * 4)
        ssq2 = ssq[:].rearrange("p a o -> p (a o)")
        p.act.activation(ssq2, ssq2, AF.Ln, bias=eps_col[0:LCH, :], scale=1.0 / 128)
        p.act.activation(ssq2, ssq2, AF.Exp, scale=-0.5)
        Hs3 = Hs.rearrange("p c h d -> p (c h) d")
        p.dve.tensor_tensor(Hs3, Hs3, ssq[:].broadcast_to([LCH, NCH * 4, 128]), ALU.mult)
        maybe_mod(l)
        for h in range(4):
            psT = PSb[6 + h % 2]
            for j in range(NCH):
                p.transpose(psT[:, j * LCH:(j + 1) * LCH], Hs[:, j, h, :], identb[0:LCH, 0:LCH], inc=(j == NCH - 1))
            p.dve.scalar_tensor_tensor(omlT[:, h, :], psT[:, :], mlnw4[:, l, h:h + 1], Gm[:, h, :], ALU.mult, ALU.mult)
        tap("omlT%d" % l, omlT[:])
        p.scope_end(mi)
        p.scope_end(mo_)

    def fft_phase(l):
        p.mark("fft%d.proj" % l)
        m = p.scope_begin()
        fxT = p.sbuf("fxT", [128, 4, 1024], BF16)
        F2 = p.sbuf("F2", [128, 4, 1024], BF16)
        Y = p.sbuf("Yf", [128, 8, 4, 2, 128], BF16)
        TB = [p.sbuf("TB%d" % i, [128, 2, 512], BF16) for i in range(3)]
        ZT = p.sbuf("ZT", [128, 4, 512], BF16)
        tg = p.sbuf("ftg", [128, 512], F32)
        p.dma("pool", wfno[:], wfno_d[l].rearrange("g c d -> c g d"))
        w = ws.get(("in", l, "fx", 0))
        for g in range(4):
            pss = proj_tile(w, g * 128, ((0, 1) if g % 2 == 0 else (2, 3)))
            if g == 1:
                ws.advance()
            for hf in range(2):
                p.act.copy(fxT[:, g, H[hf]], pss[hf][:, :])
        maybe_mod(l)
        w = ws.get(("in", l, "fg", 0))
        for g in range(4):
            pss = proj_tile(w, g * 128, ((0, 1) if g % 2 == 0 else (2, 3)))
            if g == 1:
                ws.advance()
            for hf in range(2):
                p.act.activation(tg[:], pss[hf][:, :], AF.Tanh, scale=0.5)
                p.dve.scalar_tensor_tensor(F2[:, g, H[hf]], tg[:], 1.0, pss[hf][:, :], ALU.add, ALU.mult)
        maybe_mod(l)
        p.mark("fft%d.dft" % l)
        for tt in range(8):
            for g in range(4):
                ps = PS[2 * (tt % 2) + g // 2]
                p.mm(ps[:, (g % 2) * 256:(g % 2) * 256 + 256], fxT[:, g, tt * 128:(tt + 1) * 128], tblC[:, :],
                     inc=(g % 2 == 1))
            for b in range(2):
                ps = PS[2 * (tt % 2) + b]
                dst = Y[:, tt, 2 * b:2 * b + 2, :, :].rearrange("p g s c -> p (g s c)")
                if b == 0:
                    p.act.copy(dst, ps[:, :])
                else:
                    p.dve.tensor_copy(dst, ps[:, :])
        for hf in range(2):
            for tt in range(8):
                tb = TB[(hf * 8 + tt) % 3]
                p.dma("sp", tb[:], tblT_d[:, tt * 128:(tt + 1) * 128, H[hf]].rearrange("s t u -> t s u"))
                for cs_ in range(2):
                    for g in range(4):
                        p.mm(PS[g][:, :], Y[:, tt, g, cs_, :], tb[:, cs_, :],
                             start=(tt == 0 and cs_ == 0), stop=(tt == 7 and cs_ == 1),
                             inc=(cs_ == 1 and g == 3))
            for g in range(4):
                p.act.mul(ZT[:, g, :], PS[g][:, :], 0.5)
                ps = PS[4 + g % 2]
                p.mm(ps[:, :], wfno[:, g, :], ZT[:, g, :])
                p.dve.tensor_tensor(F2[:, g, H[hf]], ps[:, :], F2[:, g, H[hf]], ALU.mult)
        p.mark("fft%d.wout" % l)
        tap("ofoT%d" % l, F2[:])
        wout_pass(l, 1, lambda k, hf: (omlT[:, k, H[hf]] if k < 4 else F2[:, k - 4, H[hf]]))
        p.scope_end(m)

    def first_mod():
        for ci in range(8):
            emit_mod_chunk(0, ci)

    for l in range(nlayers):
        phase_norm(l, between=(first_mod if l == 0 else None))
        if stop_after == ("norm", l):
            break
        if attention_phase(l):
            break
        if stop_after == ("att", l):
            break
        mlstm_phase(l)
        if stop_after == ("ml", l):
            break
        fft_phase(l)
        tap("xout%d" % l, xT[:])
    p.mark("end")
    if stop_after is not None:
        for q4 in range(4):
            p.dma("sp", yT_o[:, 4 * q4:4 * q4 + 4, :], xT[:, 4 * q4:4 * q4 + 4, :], sem="yout")
    p.finish()
    nc._marks = p.marks
    return nc


_bf = ml_dtypes.bfloat16
_CONST_CACHE = {}


def _consts(kind):
    if kind in _CONST_CACHE:
        return _CONST_CACHE[kind]
    prompt = kind == "prompt"
    c = {}
    t = np.arange(1024)
    rows, cols = t // 64, t % 64
    inv = 10000.0 ** (-np.arange(0, 64, 2, dtype=np.float64) / 64)
    C = np.ones((128, 1024)); S = np.zeros((128, 1024))
    perm = np.zeros((128, 128), np.float32)
    for d in range(128):
        blk, w = d // 64, d % 64
        pos = rows if blk == 0 else cols
        i = w % 32
        ang = pos * inv[i]
        first = w < 32
        partner = d + 32 if first else d - 32
        perm[partner, d] = 1.0
        if not prompt:
            C[d] = np.cos(ang)
            S[d] = -np.sin(ang) if first else np.sin(ang)
    c["ropeC"] = C.astype(_bf)
    c["ropeS"] = S.astype(_bf)
    c["perm"] = perm.astype(_bf)
    mb = np.zeros((128, 48), np.float32)
    if prompt:
        for kt in range(12):
            for qb in range(4):
                ok = kt >= 4 and (kt - 4) // 2 == qb
                mb[:, kt * 4 + qb] = 0.0 if ok else NEG
    c["mb"] = mb
    c["m01"] = (mb == 0.0).astype(np.float32)
    rneg = np.zeros((64, NCH), np.float32); zfl = np.full((64, NCH), -1e30, np.float32)
    keep = np.ones((64, NCH), np.float32)
    if prompt:
        for j in range(256 // LCH, NCH, 256 // LCH):
            rneg[:, j] = -1e30; zfl[:, j] = 0.0; keep[:, j] = 0.0
    c["rneg"], c["zfl"], c["keep"] = rneg, zfl, keep
    c["negbnd"] = np.full((128, 1), -1.0 if prompt else 0.0, np.float32)
    s0 = np.ones((64, 1024), np.float32); s0[:, ::LCH] = 0.0
    c["scan0"] = s0.astype(_bf)
    tri = np.zeros((LCH, 2, 1, LCH), np.float32)
    s_, l_ = np.meshgrid(np.arange(LCH), np.arange(LCH), indexing="ij")
    tri[:, 0, 0, :] = (s_ <= l_)
    tri[:, 1, 0, :] = (s_ >= l_)
    c["tri"] = tri
    m8 = np.zeros((64, 8, 1), np.float32)
    for i in range(8):
        m8[32 * (i // 4) + i % 4, i, 0] = 1.0
    c["mask8"] = m8
    T = 256 if prompt else 1024
    tt = np.arange(1024)
    same = (tt[:, None] // T) == (tt[None, :] // T)
    ang = 2 * np.pi * ((tt[:, None] % T) * (tt[None, :] % T) % T) / T
    nrm = 1.0 / np.sqrt(T * 128.0)
    tbl = np.stack([np.cos(ang) * nrm * same, -np.sin(ang) * nrm * same])
    c["tblT"] = tbl.astype(_bf)
    cc = np.arange(128)
    angc = 2 * np.pi * ((cc[:, None] * cc[None, :]) % 128) / 128
    c["tblC"] = np.concatenate([np.cos(angc), np.sin(angc)], axis=1).astype(_bf)
    _CONST_CACHE[kind] = c
    return c


def _shared(inp):
    f = lambda a: np.ascontiguousarray(a, dtype=np.float32)
    sh = {}
    sh["w_in"] = f(inp["w_in"]); sh["w_out"] = f(inp["w_out"]); sh["w_mod"] = f(inp["w_mod"])
    sh["normw"] = f(inp["norm_w"].reshape(2, 16, 128).transpose(0, 2, 1))
    sh["bmod"] = f(inp["b_mod"].reshape(2, 48, 128).transpose(0, 2, 1))
    bif = np.asarray(inp["b_if"], np.float32)
    bI = np.zeros((2, 64, 1), np.float32); bF = np.zeros((2, 64, 1), np.float32)
    bI[:, 0:4, 0] = bif[:, 0:4]; bF[:, 0:4, 0] = bif[:, 4:8]
    bI[:, 32:36, 0] = bif[:, 8:12]; bF[:, 32:36, 0] = bif[:, 12:16]
    sh["bI"], sh["bF"] = bI, bF
    sh["convw"] = f(inp["conv_w"].reshape(2, 3, 8, 128).transpose(0, 3, 1, 2))
    sh["convb"] = f(inp["conv_b"].reshape(2, 8, 128).transpose(0, 2, 1))
    sh["qnw"] = f(inp["q_norm"].reshape(2, 128, 1))
    sh["knw"] = f(inp["k_norm"].reshape(2, 128, 1))
    sh["mlnw"] = f(inp["ml_norm"].reshape(2, 4, 128).transpose(0, 2, 1))
    sh["wfno"] = f(inp["w_fno"])
    return sh


def _core_inputs(c, inp, sh):
    f = lambda a: np.ascontiguousarray(a, dtype=np.float32)
    d = dict(sh)
    if c < 4:
        d.update(_consts("prompt"))
        x = np.asarray(inp["x_prompt"])[4 * c:4 * c + 4].reshape(1024, 2048)
        cond = np.asarray(inp["c_ctx"])
        d["ckT"] = np.zeros((2, 2, 128, 512), np.float32)
        d["cv"] = np.zeros((2, 2, 512, 128), np.float32)
        d["sC"] = np.zeros((2, 2, 128, 4, 128), np.float32)
        d["sN"] = np.zeros((2, 2, 128, 4, 1), np.float32)
        d["sM"] = np.zeros((2, 64, 1), np.float32)
    else:
        b = c - 4
        d.update(_consts("sample"))
        x = np.asarray(inp["x_sample"])[b]
        cond = np.asarray(inp["c"])[b]
        d["ckT"] = f(np.asarray(inp["cache_k"])[b].transpose(0, 2, 3, 1))
        d["cv"] = f(np.asarray(inp["cache_v"])[b].transpose(0, 2, 1, 3))
        d["sC"] = f(np.asarray(inp["state_C"])[b].transpose(0, 1, 3, 2, 4))
        d["sN"] = f(np.asarray(inp["state_n"])[b].transpose(0, 1, 3, 2)[..., None])
        sm = np.asarray(inp["state_m"])[b]
        sM = np.zeros((2, 64, 1), np.float32)
        sM[:, 0:4, 0] = sm[:, 0]; sM[:, 32:36, 0] = sm[:, 1]
        d["sM"] = sM
    d["xT"] = f(x.reshape(1024, 16, 128).transpose(2, 1, 0))
    d["condT"] = f(cond.reshape(16, 128).T)
    return d


_NC_CACHE = {}


def kernel(**inputs):
    if "nc" not in _NC_CACHE:
        _NC_CACHE["nc"] = build_program()
    nc = _NC_CACHE["nc"]
    sh = _shared(inputs)
    in_maps = [_core_inputs(c, inputs, sh) for c in range(8)]
    res = run_bass_kernel_spmd(nc, in_maps, core_ids=list(range(8)))
    R = res.results
    y_prompt = np.zeros((16, 256, 2048), np.float32)
    y_sample = np.zeros((4, 1024, 2048), np.float32)
    new_k = np.zeros((16, 2, 256, 2, 128), np.float32)
    new_v = np.zeros((16, 2, 256, 2, 128), np.float32)
    new_C = np.zeros((16, 2, 2, 4, 128, 128), np.float32)
    new_n = np.zeros((16, 2, 2, 4, 128), np.float32)
    new_m = np.zeros((16, 2, 2, 4), np.float32)
    for c in range(8):
        r = R[c]
        y = np.asarray(r["yT"]).transpose(2, 1, 0).reshape(1024, 2048)
        if c >= 4:
            y_sample[c - 4] = y
            continue
        y_prompt[4 * c:4 * c + 4] = y.reshape(4, 256, 2048)
        kT = np.asarray(r["kTo"]); vo = np.asarray(r["vo"])
        Co = np.asarray(r["Co"]); No = np.asarray(r["No"]); Mo = np.asarray(r["Mo"])
        for s in range(4):
            bi = 4 * c + s
            new_k[bi] = kT[:, :, :, s * 256:(s + 1) * 256].transpose(0, 3, 1, 2)
            new_v[bi] = vo[:, :, s * 256:(s + 1) * 256, :].transpose(0, 2, 1, 3)
            new_C[bi] = Co[:, :, s].transpose(0, 1, 3, 2, 4)
            new_n[bi] = No[:, :, s, :, :, 0].transpose(0, 1, 3, 2)
            cps = 256 // LCH
            new_m[bi, :, 0, :] = Mo[:, 0:4, cps * s + cps - 1]
            new_m[bi, :, 1, :] = Mo[:, 32:36, NCH - 1 - cps * s]
    return (y_prompt, y_sample, new_k, new_v, new_C, new_n, new_m)
```
